# Optimizing a Trainium2 kernel written in Bass

```python
import math
import jax
import jax.numpy as jnp
from jax import lax
import numpy as np


D_MODEL = 2048
BATCH = 4
SEQ = 4096
DEPTH = 4

HEAD_DIM = 128
N_HEADS = D_MODEL // HEAD_DIM
A_HEADS = N_HEADS // 2
A_SUB = HEAD_DIM // 2
B_HEADS = N_HEADS - A_HEADS
A_WIDTH = A_HEADS * HEAD_DIM
B_WIDTH = B_HEADS * HEAD_DIM
AB_IN = 3 * A_WIDTH + 4 * B_WIDTH + 2 * B_HEADS
C_HEADS = N_HEADS
C_BRANCHES = ((128, 1), (512, 4), (2048, 16))
D_FF = 4 * D_MODEL
CONV_K = 4
CHUNK = 64
Q_BLOCK = 128
ROPE_THETA = 10000.0
NORM_EPS = 1e-6
N_EVEN = (DEPTH + 1) // 2
N_ODD = DEPTH // 2

kernel_name = 'hybrid_diffattn_gdn_dilated_trunk'


def rms_norm(x, g):
    xf = x.astype(jnp.float32)
    y = xf * lax.rsqrt(jnp.mean(xf * xf, axis=-1, keepdims=True) + NORM_EPS)
    return (y * g.astype(jnp.float32)).astype(x.dtype)


def l2_norm(x):
    xf = x.astype(jnp.float32)
    return xf * lax.rsqrt(jnp.sum(xf * xf, axis=-1, keepdims=True) + NORM_EPS)


def rope_tables(seq, dim):
    inv = 1.0 / (ROPE_THETA ** (jnp.arange(0, dim, 2, dtype=jnp.float32) / dim))
    ang = jnp.arange(seq, dtype=jnp.float32)[:, None] * inv[None, :]
    return jnp.cos(ang), jnp.sin(ang)


def apply_rope(x, cos, sin):
    x1, x2 = jnp.split(x.astype(jnp.float32), 2, axis=-1)
    c = cos[None, :, None, :]
    s = sin[None, :, None, :]
    return jnp.concatenate([x1 * c - x2 * s, x2 * c + x1 * s], axis=-1).astype(x.dtype)


def causal_depthwise_conv(x, w):
    k, c = w.shape
    return lax.conv_general_dilated(x, w[:, None, :].astype(x.dtype), window_strides=(1,),
                                    padding=((k - 1, 0),), dimension_numbers=('NWC', 'WIO', 'NWC'),
                                    feature_group_count=c)


def diff_attention(q, k, v, lam):
    bsz, seq = q.shape[:2]
    nblk = seq // Q_BLOCK
    scale = A_SUB ** -0.5
    kf = k.astype(jnp.float32)
    vf = v.astype(jnp.float32)
    qb = jnp.moveaxis(q.reshape(bsz, nblk, Q_BLOCK, 2 * A_HEADS, A_SUB), 1, 0)
    kpos = jnp.arange(seq)

    def one_block(args):
        qi, i = args
        s = jnp.einsum('bqhd,bkhd->bhqk', qi.astype(jnp.float32), kf) * scale
        qpos = i * Q_BLOCK + jnp.arange(Q_BLOCK)
        s = jnp.where(kpos[None, :] <= qpos[:, None], s, -jnp.inf)
        p = jax.nn.softmax(s, axis=-1).reshape(bsz, A_HEADS, 2, Q_BLOCK, seq)
        a = p[:, :, 0] - lam * p[:, :, 1]
        return jnp.einsum('bhqk,bkhd->bqhd', a, vf)

    o = lax.map(one_block, (qb, jnp.arange(nblk)))
    return jnp.moveaxis(o, 0, 1).reshape(bsz, seq, A_HEADS, HEAD_DIM)


def gated_delta_rule(q, k, v, g, beta):
    bsz, seq, nh, dk = q.shape
    dv = v.shape[-1]
    n = seq // CHUNK
    f32 = jnp.float32

    def chunks(t):
        t = t.astype(f32).reshape((bsz, n, CHUNK, nh) + t.shape[3:])
        return jnp.moveaxis(t, 3, 1)

    q = chunks(q) * dk ** -0.5
    k = chunks(k)
    v = chunks(v)
    g = chunks(g)
    beta = chunks(beta)
    gc = jnp.cumsum(g, axis=-1)
    idx = jnp.arange(CHUNK)
    causal = idx[:, None] >= idx[None, :]
    strict = idx[:, None] > idx[None, :]
    decay_incl = jnp.exp(jnp.where(causal, gc[..., :, None] - gc[..., None, :], -jnp.inf))
    decay_strict = jnp.where(strict, decay_incl, 0.0)
    kb = k * beta[..., None]
    m = jnp.einsum('bhncd,bhnjd->bhncj', kb, k) * decay_strict
    eye = jnp.eye(CHUNK, dtype=f32)
    rhs = jnp.concatenate([v * beta[..., None], kb * jnp.exp(gc)[..., None]], axis=-1)
    sol = lax.linalg.triangular_solve(eye + m, rhs, left_side=True, lower=True, unit_diagonal=True)
    u, w = sol[..., :dv], sol[..., dv:]
    attn = jnp.einsum('bhncd,bhnjd->bhncj', q, k) * decay_incl
    q_dec = q * jnp.exp(gc)[..., None]
    g_last = gc[..., -1]
    k_dec = k * jnp.exp(g_last[..., None] - gc)[..., None]

    def step(state, xs):
        u_i, w_i, attn_i, qd_i, kd_i, gl_i = xs
        v_new = u_i - jnp.einsum('bhcd,bhde->bhce', w_i, state)
        o_i = jnp.einsum('bhcd,bhde->bhce', qd_i, state) + jnp.einsum('bhcj,bhje->bhce', attn_i, v_new)
        state = state * jnp.exp(gl_i)[..., None, None] + jnp.einsum('bhcd,bhce->bhde', kd_i, v_new)
        return state, o_i

    xs = tuple(jnp.moveaxis(t, 2, 0) for t in (u, w, attn, q_dec, k_dec, g_last))
    state0 = jnp.zeros((bsz, nh, dk, dv), f32)
    _, o = lax.scan(step, state0, xs)
    return o.transpose(1, 0, 3, 2, 4).reshape(bsz, seq, nh, dv)


def diff_delta_mixer(h, layer_idx, w_in, a_q_norm, a_k_norm, a_lambda, a_sub_norm,
                     b_conv, b_a_log, b_dt_bias, b_out_norm, w_out, cos_a, sin_a):
    bsz, seq, _ = h.shape
    f32 = jnp.float32
    proj = h @ w_in
    cuts = [A_WIDTH, 2 * A_WIDTH, 3 * A_WIDTH, 3 * A_WIDTH + 3 * B_WIDTH,
            3 * A_WIDTH + 4 * B_WIDTH, 3 * A_WIDTH + 4 * B_WIDTH + B_HEADS]
    aq, ak, av, bqkv, bz, ba, bb = jnp.split(proj, cuts, axis=-1)
    aq = apply_rope(rms_norm(aq.reshape(bsz, seq, 2 * A_HEADS, A_SUB), a_q_norm), cos_a, sin_a)
    ak = apply_rope(rms_norm(ak.reshape(bsz, seq, 2 * A_HEADS, A_SUB), a_k_norm), cos_a, sin_a)
    av = av.reshape(bsz, seq, A_HEADS, HEAD_DIM)
    lam_init = 0.8 - 0.6 * math.exp(-0.3 * layer_idx)
    lv = a_lambda.astype(f32)
    lam = jnp.exp(jnp.sum(lv[0] * lv[1])) - jnp.exp(jnp.sum(lv[2] * lv[3])) + lam_init
    oa = diff_attention(aq, ak, av, lam)
    oa = (rms_norm(oa, a_sub_norm) * (1.0 - lam_init)).astype(h.dtype)
    bqkv = jax.nn.silu(causal_depthwise_conv(bqkv, b_conv))
    bq, bk, bv = jnp.split(bqkv, 3, axis=-1)
    bq = l2_norm(bq.reshape(bsz, seq, B_HEADS, HEAD_DIM))
    bk = l2_norm(bk.reshape(bsz, seq, B_HEADS, HEAD_DIM))
    bv = bv.reshape(bsz, seq, B_HEADS, HEAD_DIM)
    g = -jnp.exp(b_a_log.astype(f32)) * jax.nn.softplus((ba + b_dt_bias).astype(f32))
    beta = jax.nn.sigmoid(bb.astype(f32))
    ob = gated_delta_rule(bq, bk, bv, g, beta)
    ob = rms_norm(ob, b_out_norm) * jax.nn.silu(bz.reshape(bsz, seq, B_HEADS, HEAD_DIM).astype(f32))
    ob = ob.astype(h.dtype)
    o = jnp.concatenate([oa.reshape(bsz, seq, A_WIDTH), ob.reshape(bsz, seq, B_WIDTH)], axis=-1)
    return o @ w_out


def dilated_branch(q, k, v, window, dil):
    bsz, seq, nh, hd = q.shape
    hops = window // dil
    s_pad = -(-seq // (dil * hops)) * (dil * hops)
    sub_len = s_pad // dil
    n = sub_len // hops
    pad = ((0, 0), (0, s_pad - seq), (0, 0), (0, 0))

    def strided(t):
        t = jnp.pad(t.astype(jnp.float32), pad).reshape(bsz, sub_len, dil, nh, hd)
        return t.transpose(0, 2, 3, 1, 4).reshape(bsz, dil, nh, n, hops, hd)

    def with_prev(t):
        prev = jnp.pad(t, ((0, 0), (0, 0), (0, 0), (1, 0), (0, 0), (0, 0)))[:, :, :, :-1]
        return jnp.concatenate([prev, t], axis=4)

    qs = strided(q)
    kb = with_prev(strided(k))
    vb = with_prev(strided(v))
    s = jnp.einsum('brhnqd,brhnkd->brhnqk', qs, kb) * hd ** -0.5
    a = jnp.arange(hops)[:, None]
    b = jnp.arange(2 * hops)[None, :]
    dist = a + hops - b
    band = (dist >= 0) & (dist <= hops)
    first = (jnp.arange(n) == 0)[:, None, None]
    mask = band[None] & ~(first & (b < hops)[None])
    s = jnp.where(mask, s, -jnp.inf)
    mx = jnp.max(s, axis=-1, keepdims=True)
    p = jnp.exp(s - mx)
    den = jnp.sum(p, axis=-1, keepdims=True)
    o = jnp.einsum('brhnqk,brhnkd->brhnqd', p, vb) / den
    lse = (mx + jnp.log(den))[..., 0]
    o = o.reshape(bsz, dil, nh, sub_len, hd).transpose(0, 3, 1, 2, 4).reshape(bsz, s_pad, nh, hd)
    lse = lse.reshape(bsz, dil, nh, sub_len).transpose(0, 3, 1, 2).reshape(bsz, s_pad, nh)
    return o[:, :seq], lse[:, :seq]


def dilated_mixer(h, w_in, q_norm, k_norm, w_out, cos_c, sin_c):
    bsz, seq, _ = h.shape
    q, k, v = jnp.split(h @ w_in, 3, axis=-1)
    q = apply_rope(rms_norm(q.reshape(bsz, seq, C_HEADS, HEAD_DIM), q_norm), cos_c, sin_c)
    k = apply_rope(rms_norm(k.reshape(bsz, seq, C_HEADS, HEAD_DIM), k_norm), cos_c, sin_c)
    v = v.reshape(bsz, seq, C_HEADS, HEAD_DIM)
    outs = []
    lses = []
    for window, dil in C_BRANCHES:
        o_g, lse_g = dilated_branch(q, k, v, window, dil)
        outs.append(o_g)
        lses.append(lse_g)
    wts = jax.nn.softmax(jnp.stack(lses, axis=0), axis=0)
    o = jnp.einsum('gbsh,gbshd->bshd', wts, jnp.stack(outs, axis=0))
    return o.reshape(bsz, seq, C_HEADS * HEAD_DIM).astype(h.dtype) @ w_out


def squared_relu_mlp(h, w1, w2):
    return jnp.square(jax.nn.relu(h @ w1)) @ w2


def setup_inputs(seed: int = 0) -> dict:
    key = jax.random.key(seed)
    ks = jax.random.split(key, 20)
    f32 = jnp.float32

    def nrm(k, shape, scale):
        return jax.random.normal(k, shape, f32) * scale

    def gain(k, shape):
        return 1.0 + 0.02 * jax.random.normal(k, shape, f32)

    dt = jnp.exp(jax.random.uniform(ks[9], (N_EVEN, B_HEADS), f32, math.log(1e-3), math.log(1e-1)))
    return {
        'x': nrm(ks[0], (BATCH, SEQ, D_MODEL), 1.0),
        'ab_norm': gain(ks[1], (N_EVEN, D_MODEL)),
        'ab_w_in': nrm(ks[2], (N_EVEN, D_MODEL, AB_IN), D_MODEL ** -0.5),
        'a_q_norm': gain(ks[3], (N_EVEN, A_SUB)),
        'a_k_norm': gain(ks[4], (N_EVEN, A_SUB)),
        'a_lambda': nrm(ks[5], (N_EVEN, 4, A_SUB), 0.1),
        'a_sub_norm': gain(ks[6], (N_EVEN, HEAD_DIM)),
        'b_conv': nrm(ks[7], (N_EVEN, CONV_K, 3 * B_WIDTH), CONV_K ** -0.5),
        'b_a_log': jnp.log(jax.random.uniform(ks[8], (N_EVEN, B_HEADS), f32, 1.0, 16.0)),
        'b_dt_bias': dt + jnp.log(-jnp.expm1(-dt)),
        'b_out_norm': gain(ks[10], (N_EVEN, HEAD_DIM)),
        'ab_w_out': nrm(ks[11], (N_EVEN, D_MODEL, D_MODEL), D_MODEL ** -0.5),
        'c_norm': gain(ks[12], (N_ODD, D_MODEL)),
        'c_w_in': nrm(ks[13], (N_ODD, D_MODEL, 3 * C_HEADS * HEAD_DIM), D_MODEL ** -0.5),
        'c_q_norm': gain(ks[14], (N_ODD, HEAD_DIM)),
        'c_k_norm': gain(ks[15], (N_ODD, HEAD_DIM)),
        'c_w_out': nrm(ks[16], (N_ODD, C_HEADS * HEAD_DIM, D_MODEL), D_MODEL ** -0.5),
        'mlp_norm': gain(ks[17], (DEPTH, D_MODEL)),
        'mlp_w1': nrm(ks[18], (DEPTH, D_MODEL, D_FF), D_MODEL ** -0.5),
        'mlp_w2': nrm(ks[19], (DEPTH, D_FF, D_MODEL), D_FF ** -0.5),
    }


def reference(x, ab_norm, ab_w_in, a_q_norm, a_k_norm, a_lambda, a_sub_norm, b_conv, b_a_log,
              b_dt_bias, b_out_norm, ab_w_out, c_norm, c_w_in, c_q_norm, c_k_norm, c_w_out,
              mlp_norm, mlp_w1, mlp_w2):
    seq = x.shape[1]
    cos_a, sin_a = rope_tables(seq, A_SUB)
    cos_c, sin_c = rope_tables(seq, HEAD_DIM)
    for l in range(DEPTH):
        i = l // 2
        if l % 2 == 0:
            x = x + diff_delta_mixer(rms_norm(x, ab_norm[i]), l, ab_w_in[i], a_q_norm[i], a_k_norm[i],
                                     a_lambda[i], a_sub_norm[i], b_conv[i], b_a_log[i], b_dt_bias[i],
                                     b_out_norm[i], ab_w_out[i], cos_a, sin_a)
        else:
            x = x + dilated_mixer(rms_norm(x, c_norm[i]), c_w_in[i], c_q_norm[i], c_k_norm[i],
                                  c_w_out[i], cos_c, sin_c)
        x = x + squared_relu_mlp(rms_norm(x, mlp_norm[l]), mlp_w1[l], mlp_w2[l])
    return x
```

```python
import contextlib
import math
import os as _os
import numpy as np
import ml_dtypes
import concourse.bass as bass
import concourse.mybir as mybir
from concourse.bass_utils import run_bass_kernel_spmd

F32 = mybir.dt.float32
BF16 = mybir.dt.bfloat16
AF = mybir.ActivationFunctionType
ALU = mybir.AluOpType
AX = mybir.AxisListType

D = 2048
SEQ = 4096
NB_CORE = 8
EPS = 1e-6

ENGS = ("pe", "act", "dve", "pool", "sp")
CHUNK = 16000


class Buf:
    __slots__ = ("name", "last_write", "readers", "dma_sem", "dma_count", "uid")

    _n = 0

    def __init__(self, name):
        Buf._n += 1
        self.uid = Buf._n
        self.name = name
        self.last_write = None
        self.readers = {}
        self.dma_sem = None
        self.dma_count = 0


class Tok:
    __slots__ = ("key", "val", "eng")

    def __init__(self, key, val, eng):
        self.key = key
        self.val = val
        self.eng = eng


class Prog:
    def __init__(self, nc):
        self.nc = nc
        self.stack = contextlib.ExitStack()
        self.semstack = contextlib.ExitStack()
        self.ops = {e: [] for e in ENGS}
        self.count = {e: 0 for e in ENGS}
        self.waited = {e: {} for e in ENGS}
        self.sems = {}
        self.dma_final = {}
        self.nsem = 0
        self.ntile = 0

    def sem(self, key):
        if key not in self.sems:
            self.nsem += 1
            self.sems[key] = self.semstack.enter_context(self.nc.semaphore("s%d" % self.nsem))
        return self.sems[key]

    def sbuf(self, name, shape, dt):
        self.ntile += 1
        t = self.stack.enter_context(self.nc.sbuf_tensor("%s_%d" % (name, self.ntile), list(shape), dt))
        return t, Buf(name)

    def psum(self, name, shape, dt):
        t = self.stack.enter_context(self.nc.psum_tensor(name, list(shape), dt))
        return t, Buf(name)

    def _collect(self, eng, reads, writes):
        waits = {}

        def add(tok, allow_same):
            if tok is None:
                return
            if tok.eng == eng and not allow_same:
                return
            cur = waits.get(tok.key)
            if cur is None or cur < tok.val:
                waits[tok.key] = tok.val

        for b in reads:
            add(b.last_write, True)
        for b in writes:
            add(b.last_write, False)
            for t in b.readers.values():
                add(t, False)
        out = []
        wd = self.waited[eng]
        for k, v in waits.items():
            if wd.get(k, 0) >= v:
                continue
            wd[k] = v
            out.append((k, v))
        return out

    def _commit(self, tok, reads, writes):
        for b in reads:
            cur = b.readers.get(tok.key)
            if cur is None or cur.val < tok.val:
                b.readers[tok.key] = tok
        for b in writes:
            b.last_write = tok
            b.readers = {}

    def op(self, eng, fn, reads=(), writes=(), acc=False):
        reads = list(reads)
        writes = list(writes)
        waits = self._collect(eng, reads, [] if acc else writes)
        self.count[eng] += 1
        n = self.count[eng]
        key = ("e", eng, (n - 1) // CHUNK)
        tok = Tok(key, (n - 1) % CHUNK + 1, eng)
        self.sem(key)
        self.ops[eng].append((waits, fn, key, 1))
        self._commit(tok, reads, writes)
        return tok

    def dma(self, eng, fn, reads=(), writes=(), sembuf=None):
        reads = list(reads)
        writes = list(writes)
        waits = self._collect(eng, reads, writes)
        sb = sembuf if sembuf is not None else (writes[0] if writes else reads[0])
        if sb.dma_sem is None:
            sb.dma_sem = ("d", sb.uid)
            self.sem(sb.dma_sem)
        sb.dma_count += 1
        tok = Tok(sb.dma_sem, 16 * sb.dma_count, None)
        self.dma_final[sb.dma_sem] = tok.val
        self.ops[eng].append((waits, fn, sb.dma_sem, 16))
        self._commit(tok, reads, writes)
        return tok

    def push_scope(self):
        self._outer = getattr(self, "_outer", [])
        self._outer.append(self.stack)
        self.stack = contextlib.ExitStack()

    def pop_scope(self):
        self.barrier()
        self.stack.close()
        self.stack = self._outer.pop()

    def barrier(self):
        toks = []
        for e in ENGS:
            n = self.count[e]
            if n:
                toks.append((("e", e, (n - 1) // CHUNK), (n - 1) % CHUNK + 1, e))
        for e in ENGS:
            wd = self.waited[e]
            waits = []
            for k, v in self.dma_final.items():
                if wd.get(k, 0) < v:
                    wd[k] = v
                    waits.append((k, v))
            for k, v, src in toks:
                if src != e and wd.get(k, 0) < v:
                    wd[k] = v
                    waits.append((k, v))
            if waits:
                self.ops[e].append((waits, None, None, 0))

    def fence(self, eng="sp"):
        waits = []
        wd = self.waited[eng]
        for k, v in self.dma_final.items():
            if wd.get(k, 0) < v:
                wd[k] = v
                waits.append((k, v))
        self.ops[eng].append((waits, None, None, 0))

    def finish(self):
        waits = []
        wd = self.waited["sp"]
        for k, v in self.dma_final.items():
            if wd.get(k, 0) < v:
                waits.append((k, v))
        for e in ENGS:
            n = self.count[e]
            if n:
                waits.append((("e", e, (n - 1) // CHUNK), (n - 1) % CHUNK + 1))
        self.ops["sp"].append((waits, None, None, 0))
        with self.nc.Block() as block:
            def mk(e):
                def body(engine):
                    for ws, fn, key, inc in self.ops[e]:
                        for k, v in ws:
                            engine.wait_ge(self.sems[k], v)
                        if fn is None:
                            continue
                        fn(engine).then_inc(self.sems[key], inc)
                return body
            block.tensor(mk("pe"))
            block.scalar(mk("act"))
            block.vector(mk("dve"))
            block.gpsimd(mk("pool"))
            block.sync(mk("sp"))
        self.stack.close()
        self.semstack.close()


class Ring:
    def __init__(self, P, name, shape, dt, n):
        self.items = [P.sbuf(name, shape, dt) for _ in range(n)]
        self.i = 0

    def get(self):
        it = self.items[self.i % len(self.items)]
        self.i += 1
        return it


class Env:
    def __init__(self, nc):
        self.nc = nc
        self.P = Prog(nc)
        self.banks = [self.P.psum("ps%d" % i, [128, 512], F32) for i in range(8)]
        self.din = {}
        self.dout = {}

    def inp(self, name, shape, dt=F32):
        ap = self.nc.dram_tensor(name, list(shape), dt, kind="ExternalInput").ap()
        self.din[name] = ap
        return ap

    def out(self, name, shape, dt=F32):
        ap = self.nc.dram_tensor(name, list(shape), dt, kind="ExternalOutput").ap()
        self.dout[name] = ap
        return ap

    def dump(self, name, ap, buf):
        if not _os.environ.get("DBG_DUMP") or ("dbg_" + name) in self.dout:
            return
        o = self.out("dbg_" + name, list(ap.shape), ap.dtype)
        self.P.dma("sp", lambda e: e.dma_start(out=o, in_=ap), reads=[buf])

    def load_ap(self, name, src, shape, dt=F32):
        t, b = self.P.sbuf(name, shape, dt)
        self.P.dma("sp", lambda e: e.dma_start(out=t[:], in_=src, allow_slow_non_contiguous=True), writes=[b])
        return t, b

    def load_const(self, name, shape, dt=F32, cast=None):
        P = self.P
        src = self.inp(name, shape, dt)
        t, b = P.sbuf(name, shape, dt)
        P.dma("sp", lambda e: e.dma_start(out=t[:], in_=src), writes=[b])
        if cast is None:
            return t, b
        t2, b2 = P.sbuf(name + "c", shape, cast)
        P.op("dve", lambda e: e.tensor_copy(t2[:], t[:]), reads=[b], writes=[b2])
        return t2, b2


def build_xt(env, xt, xtb, load_src, nblk, KC, ident, tok0=0):
    P = env.P
    n = 0
    for blk in range(nblk):
        for g in range(KC // 8):
            src, sb = load_src(blk, g)
            pst, psb = env.banks[n % 8]
            n += 1
            psv = pst[:].bitcast(BF16).rearrange("p (a b) -> p a b", b=128)

            def tr(e, src=src, psv=psv):
                ins = None
                for j in range(8):
                    ins = e.transpose(psv[:, j, :], src[:, j * 128:(j + 1) * 128], ident[0][:])
                return ins
            P.op("pe", tr, reads=[sb, ident[1]], writes=[psb])
            dst = xt[:, g * 8:(g + 1) * 8, tok0 + blk * 128:tok0 + (blk + 1) * 128]
            if n % 2 == 0:
                P.op("act", lambda e, dst=dst, psv=psv: e.copy(dst, psv), reads=[psb], writes=[xtb])
            else:
                P.op("dve", lambda e, dst=dst, psv=psv: e.tensor_copy(dst, psv), reads=[psb], writes=[xtb])


def linear(env, xt, xtb, nblk, KC, w_ap, groups, stage_ring, wb_ring, cast_engs=("pool",)):
    P = env.P
    wv = w_ap.rearrange("(kc p) n -> p kc n", p=128)
    npiece = KC // 4
    bgs = [list(range(i, min(i + 4, nblk))) for i in range(0, nblk, 4)]
    multi = len(bgs) > 1
    nring = len(wb_ring.items)
    assert npiece <= nring or not multi
    plist = [(gi, pi) for gi in range(len(groups)) for pi in range(npiece)]
    loaded = []
    state = {"ncast": 0, "bank_set": 0}

    def ensure(upto):
        upto = min(upto, len(plist) - 1)
        while len(loaded) <= upto:
            gi, pi = plist[len(loaded)]
            c0, width = groups[gi][0], groups[gi][1]
            st, stb = stage_ring.get()
            P.dma("sp", lambda e, st=st, pi=pi, c0=c0, width=width: e.dma_start(
                out=st[:, :, :width], in_=wv[:, pi * 4:(pi + 1) * 4, c0:c0 + width]), writes=[stb])
            wb, wbb = wb_ring.get()
            ce = cast_engs[state["ncast"] % len(cast_engs)]
            state["ncast"] += 1
            if ce == "act":
                P.op("act", lambda e, wb=wb, st=st, width=width: e.copy(wb[:, :, :width], st[:, :, :width]),
                     reads=[stb], writes=[wbb])
            else:
                P.op(ce, lambda e, wb=wb, st=st, width=width: e.tensor_copy(wb[:, :, :width], st[:, :, :width]),
                     reads=[stb], writes=[wbb])
            loaded.append((wb, wbb))

    for gi, (c0, width, epi, pre) in enumerate(groups):
        gstart = gi * npiece
        for bg in bgs:
            banks = [env.banks[state["bank_set"] * 4 + i] for i in range(len(bg))]
            state["bank_set"] ^= 1
            ctxs = [pre(blk) if pre is not None else None for blk in bg]
            for pi in range(npiece):
                ensure((gstart if multi else gstart + pi) + nring - 1)
                wb, wbb = loaded[gstart + pi]

                def mm(e, wb=wb, pi=pi, bg=bg, banks=banks, width=width):
                    ins = None
                    for bi, blk in enumerate(bg):
                        for j in range(4):
                            kc = pi * 4 + j
                            ins = e.matmul(banks[bi][0][:, :width],
                                           lhsT=xt[:, kc, blk * 128:(blk + 1) * 128],
                                           rhs=wb[:, j, :width],
                                           start=(kc == 0), stop=(kc == KC - 1))
                    return ins
                P.op("pe", mm, reads=[xtb, wbb], writes=[b for _, b in banks], acc=(pi > 0))
            for bi, blk in enumerate(bg):
                epi(banks[bi][0][:, :width], banks[bi][1], blk, ctxs[bi])


def make_norm_loader(env, x_ap, g_tile, xring, hring, sring, junk):
    P = env.P
    cache = {}

    def load(blk, g):
        if blk in cache:
            ht, hb = cache[blk]
            return ht[:, g * 1024:(g + 1) * 1024], hb
        xt_, xb = xring.get()
        P.dma("sp", lambda e: e.dma_start(out=xt_[:], in_=x_ap[blk * 128:(blk + 1) * 128, :]), writes=[xb])
        st, sb = sring.get()
        ht, hb = hring.get()
        P.op("act", lambda e: e.activation(out=ht[:], in_=xt_[:], func=AF.Square,
                                           accum_out=st[:, 0:1]), reads=[xb], writes=[hb, sb])
        P.op("act", lambda e: e.activation(out=st[:, 1:2], in_=st[:, 0:1], func=AF.Sqrt,
                                           scale=1.0 / D, bias=env.eps[0][:, 0:1]),
             reads=[sb, env.eps[1]], writes=[sb])
        P.op("dve", lambda e: e.reciprocal(st[:, 2:3], st[:, 1:2]), reads=[sb], writes=[sb])
        P.op("dve", lambda e: e.scalar_tensor_tensor(out=ht[:], in0=xt_[:], scalar=st[:, 2:3],
                                                    in1=g_tile[0][:], op0=ALU.mult, op1=ALU.mult),
             reads=[xb, sb, g_tile[1]], writes=[hb])
        cache.clear()
        cache[blk] = (ht, hb)
        return ht[:, g * 1024:(g + 1) * 1024], hb
    return load


def make_bf16_loader(env, x_ap, ring, tok0=0):
    P = env.P

    def load(blk, g):
        t, b = ring.get()
        r0 = tok0 + blk * 128
        P.dma("sp", lambda e: e.dma_start(out=t[:], in_=x_ap[r0:r0 + 128, g * 1024:(g + 1) * 1024]), writes=[b])
        return t[:], b
    return load


def epi_store(env, out_ap, c_off, ring, eng="act"):
    P = env.P

    def epi(ps, psb, blk, c0, ctx=None):
        width = ps.shape[1]
        t, b = ring.get()
        if eng == "act":
            P.op("act", lambda e: e.copy(t[:, :width], ps), reads=[psb], writes=[b])
        else:
            P.op("dve", lambda e: e.tensor_copy(t[:, :width], ps), reads=[psb], writes=[b])
        P.dma("sp", lambda e: e.dma_start(out=out_ap[blk * 128:(blk + 1) * 128, c0 - c_off:c0 - c_off + width],
                                          in_=t[:, :width]), reads=[b])
    return epi


def epi_qknorm_rope(env, out_ap, c_off, hd, gc_tab, t_tab, rings):
    P = env.P
    half = hd // 2
    sq_ring, s_ring, y_ring, o_ring = rings

    def epi(ps, psb, blk, c0, ctx=None):
        width = ps.shape[1]
        nh = width // hd
        sq, sqb = sq_ring.get()
        P.op("act", lambda e: e.activation(out=sq[:, :width], in_=ps, func=AF.Square), reads=[psb], writes=[sqb])
        st, sb = s_ring.get()
        P.op("dve", lambda e: e.tensor_reduce(out=st[:, 0, :nh], in_=sq[:, :width].rearrange("p (h d) -> p h d", d=hd),
                                              axis=AX.X, op=ALU.add), reads=[sqb], writes=[sb])
        P.op("act", lambda e: e.activation(out=st[:, 1, :nh], in_=st[:, 0, :nh], func=AF.Sqrt,
                                           scale=1.0 / hd, bias=env.eps[0][:, 0:1]),
             reads=[sb, env.eps[1]], writes=[sb])
        P.op("dve", lambda e: e.reciprocal(st[:, 2, :nh], st[:, 1, :nh]), reads=[sb], writes=[sb])
        y, yb = y_ring.get()
        y3 = y[:, 0, :width].rearrange("p (h d) -> p h d", d=hd)
        w3 = y[:, 1, :width].rearrange("p (h d) -> p h d", d=hd)
        ps3 = ps.rearrange("p (h d) -> p h d", d=hd)
        P.op("dve", lambda e: e.tensor_tensor(out=y3, in0=ps3,
                                              in1=st[:, 2, :nh].unsqueeze(2).broadcast_to([128, nh, hd]),
                                              op=ALU.mult), reads=[psb, sb], writes=[yb])
        tt = t_tab[0]
        P.op("pool", lambda e: e.tensor_tensor(out=w3[:, :, 0:half], in0=y3[:, :, half:hd],
                                               in1=tt[:, blk, 0:half].unsqueeze(1).broadcast_to([128, nh, half]),
                                               op=ALU.mult), reads=[yb, t_tab[1]], writes=[yb])
        P.op("pool", lambda e: e.tensor_tensor(out=w3[:, :, half:hd], in0=y3[:, :, 0:half],
                                               in1=tt[:, blk, half:hd].unsqueeze(1).broadcast_to([128, nh, half]),
                                               op=ALU.mult), reads=[yb, t_tab[1]], writes=[yb])
        P.op("dve", lambda e: e.tensor_tensor(out=y3, in0=y3,
                                              in1=gc_tab[0][:, blk, :].unsqueeze(1).broadcast_to([128, nh, hd]),
                                              op=ALU.mult), reads=[yb, gc_tab[1]], writes=[yb])
        o, ob = o_ring.get()
        P.op("dve", lambda e: e.tensor_tensor(out=o[:, :width], in0=y[:, 0, :width], in1=y[:, 1, :width],
                                              op=ALU.add), reads=[yb], writes=[ob])
        P.dma("sp", lambda e: e.dma_start(out=out_ap[blk * 128:(blk + 1) * 128, c0 - c_off:c0 - c_off + width],
                                          in_=o[:, :width]), reads=[ob])
    return epi


def build_rope_tables(env, nblk, hd):
    P = env.P
    half = hd // 2
    outs = []
    for name in ("q", "k"):
        outs.append(P.sbuf(name + "gc", [128, nblk, hd], F32))
        outs.append(P.sbuf(name + "tt", [128, nblk, hd], F32))
    P.push_scope()
    cfull = env.load_const("cfull", [128, nblk, hd])
    ssig = env.load_const("ssig", [128, nblk, hd])
    for i, name in enumerate(("gq", "gk")):
        g_tile = env.load_const(name, [128, hd])
        g = g_tile[0]
        gc, tt = outs[2 * i], outs[2 * i + 1]
        P.op("dve", lambda e, gc=gc, g=g: e.tensor_tensor(out=gc[0][:], in0=cfull[0][:],
                                              in1=g[:, 0:hd].unsqueeze(1).broadcast_to([128, nblk, hd]), op=ALU.mult),
             reads=[cfull[1], g_tile[1]], writes=[gc[1]])
        P.op("dve", lambda e, tt=tt, g=g: e.tensor_tensor(out=tt[0][:, :, 0:half], in0=ssig[0][:, :, 0:half],
                                              in1=g[:, half:hd].unsqueeze(1).broadcast_to([128, nblk, half]), op=ALU.mult),
             reads=[ssig[1], g_tile[1]], writes=[tt[1]])
        P.op("dve", lambda e, tt=tt, g=g: e.tensor_tensor(out=tt[0][:, :, half:hd], in0=ssig[0][:, :, half:hd],
                                              in1=g[:, 0:half].unsqueeze(1).broadcast_to([128, nblk, half]), op=ALU.mult),
             reads=[ssig[1], g_tile[1]], writes=[tt[1]])
    P.pop_scope()
    return outs


def common_consts(env):
    P = env.P
    env.ident = env.load_const("ident", [128, 128], F32, cast=BF16)
    env.eps = P.sbuf("eps", [128, 1], F32)
    P.op("dve", lambda e: e.memset(env.eps[0][:], EPS), writes=[env.eps[1]])


def prog_in_even(NT):
    nc = bass.Bass("TRN2", target_bir_lowering=False)
    env = Env(nc)
    P = env.P
    nblk = NT // 128
    KC = D // 128
    x = env.inp("x", [NT, D])
    w = env.inp("w", [D, 7184])
    common_consts(env)
    g_tile = env.load_const("g", [128, D])
    qa = env.out("qa", [NT, 1024], BF16)
    ka = env.out("ka", [NT, 1024], BF16)
    va = env.out("va", [NT, 1024], BF16)
    bqkv = env.out("bqkv", [NT, 3072], F32)
    bz = env.out("bz", [NT, 1024], F32)
    bab = env.out("bab", [NT, 16], F32)
    gcq, ttq, gck, ttk = build_rope_tables(env, nblk, 64)

    xt, xtb = P.sbuf("xT", [128, KC, NT], BF16)
    xring = Ring(P, "xin", [128, D], F32, 2)
    hring = Ring(P, "hbf", [128, D], BF16, 2)
    sring = Ring(P, "nst", [128, 4], F32, 2)
    junk = None
    build_xt(env, xt, xtb, make_norm_loader(env, x, g_tile, xring, hring, sring, junk), nblk, KC, env.ident)

    stage = Ring(P, "wst", [128, 4, 512], F32, 3)
    wbr = Ring(P, "wbf", [128, 4, 512], BF16, 8)
    rings = (Ring(P, "sq", [128, 512], F32, 2), Ring(P, "st", [128, 3, 16], F32, 2),
             Ring(P, "y", [128, 2, 512], F32, 2), Ring(P, "o", [128, 512], BF16, 3))
    o32 = Ring(P, "o32", [128, 512], F32, 3)
    o16 = rings[3]
    eq = epi_qknorm_rope(env, qa, 0, 64, gcq, ttq, rings)
    ek = epi_qknorm_rope(env, ka, 1024, 64, gck, ttk, rings)
    ev = epi_store(env, va, 2048, o16, "act")
    eb = epi_store(env, bqkv, 3072, o32, "act")
    ez = epi_store(env, bz, 6144, o32, "act")
    eab = epi_store(env, bab, 7168, o32, "act")
    groups = []
    for c0 in range(0, 7184, 512):
        width = min(512, 7184 - c0)
        f = eq if c0 < 1024 else ek if c0 < 2048 else ev if c0 < 3072 else eb if c0 < 6144 else ez if c0 < 7168 else eab
        groups.append((c0, width, (lambda ps, psb, blk, ctx, f=f, c0=c0: f(ps, psb, blk, c0)), None))
    linear(env, xt, xtb, nblk, KC, w, groups, stage, wbr)
    P.finish()
    return nc


def epi_resid(env, res_ap, out_ap, rring, oring, tok0=0):
    P = env.P

    def pre(blk, c0, width):
        r, rb = rring.get()
        r0 = tok0 + blk * 128
        P.dma("sp", lambda e: e.dma_start(out=r[:, :width], in_=res_ap[r0:r0 + 128, c0:c0 + width]), writes=[rb])
        return r, rb

    def epi(ps, psb, blk, c0, ctx):
        width = ps.shape[1]
        r, rb = ctx
        t, b = oring.get()
        P.op("dve", lambda e: e.tensor_tensor(out=t[:, :width], in0=ps, in1=r[:, :width], op=ALU.add),
             reads=[psb, rb], writes=[b])
        r0 = tok0 + blk * 128
        P.dma("sp", lambda e: e.dma_start(out=out_ap[r0:r0 + 128, c0:c0 + width], in_=t[:, :width]), reads=[b])
    return pre, epi


def epi_relu2(env, out_ap, tring, oring):
    P = env.P

    def epi(ps, psb, blk, c0, ctx=None):
        width = ps.shape[1]
        t, tb = tring.get()
        P.op("act", lambda e: e.activation(out=t[:, :width], in_=ps, func=AF.Relu), reads=[psb], writes=[tb])
        o, ob = oring.get()
        P.op("dve", lambda e: e.tensor_tensor(out=o[:, :width], in0=t[:, :width], in1=t[:, :width], op=ALU.mult),
             reads=[tb], writes=[ob])
        P.dma("sp", lambda e: e.dma_start(out=out_ap[blk * 128:(blk + 1) * 128, c0:c0 + width], in_=o[:, :width]),
              reads=[ob])
    return epi


def prog_in_odd(NT):
    nc = bass.Bass("TRN2", target_bir_lowering=False)
    env = Env(nc)
    P = env.P
    nblk = NT // 128
    KC = D // 128
    x = env.inp("x", [NT, D])
    w = env.inp("w", [D, 6144])
    common_consts(env)
    g_tile = env.load_const("g", [128, D])
    qc = env.out("qc", [NT, 2048], BF16)
    kc_ = env.out("kc", [NT, 2048], BF16)
    vc = env.out("vc", [NT, 2048], BF16)
    gcq, ttq, gck, ttk = build_rope_tables(env, nblk, 128)
    xt, xtb = P.sbuf("xT", [128, KC, NT], BF16)
    xring = Ring(P, "xin", [128, D], F32, 2)
    hring = Ring(P, "hbf", [128, D], BF16, 2)
    sring = Ring(P, "nst", [128, 4], F32, 2)
    junk = None
    build_xt(env, xt, xtb, make_norm_loader(env, x, g_tile, xring, hring, sring, junk), nblk, KC, env.ident)
    stage = Ring(P, "wst", [128, 4, 512], F32, 3)
    wbr = Ring(P, "wbf", [128, 4, 512], BF16, 8)
    rings = (Ring(P, "sq", [128, 512], F32, 2), Ring(P, "st", [128, 3, 16], F32, 2),
             Ring(P, "y", [128, 2, 512], F32, 2), Ring(P, "o", [128, 512], BF16, 3))
    eq = epi_qknorm_rope(env, qc, 0, 128, gcq, ttq, rings)
    ek = epi_qknorm_rope(env, kc_, 2048, 128, gck, ttk, rings)
    ev = epi_store(env, vc, 4096, rings[3], "act")
    groups = []
    for c0 in range(0, 6144, 512):
        f = eq if c0 < 2048 else ek if c0 < 4096 else ev
        groups.append((c0, 512, (lambda ps, psb, blk, ctx, f=f, c0=c0: f(ps, psb, blk, c0)), None))
    linear(env, xt, xtb, nblk, KC, w, groups, stage, wbr)
    P.finish()
    return nc


def prog_post(NT, TT2):
    nc = bass.Bass("TRN2", target_bir_lowering=False)
    env = Env(nc)
    P = env.P
    nblk = NT // 128
    KC = D // 128
    oT_in = env.inp("oT", [D, NT], BF16)
    x = env.inp("x", [NT, D])
    wo = env.inp("wo", [D, D])
    w1 = env.inp("w1", [D, 4 * D])
    w2 = env.inp("w2", [4 * D, D])
    common_consts(env)
    g_tile = env.load_const("g", [128, D])
    xout = env.out("xout", [NT, D])
    x1 = nc.dram_tensor("x1s", [NT, D], F32, kind="Internal").ap()
    a_s = nc.dram_tensor("as", [NT, 4 * D], BF16, kind="Internal").ap()

    xt_full, xtb = P.sbuf("xT", [128, 64 * 512], BF16)
    xt16 = xt_full[:, :KC * NT].rearrange("p (k t) -> p k t", k=KC)
    stage = Ring(P, "wst", [128, 4, 512], F32, 3)
    wbr = Ring(P, "wbf", [128, 4, 512], BF16, 8)
    ldring = Ring(P, "ld16", [128, 1024], BF16, 3)
    rring = Ring(P, "res", [128, 512], F32, 5)
    oring = Ring(P, "o32", [128, 512], F32, 3)
    tring = Ring(P, "t32", [128, 512], F32, 2)
    o16 = Ring(P, "o16", [128, 512], BF16, 3)
    xring = Ring(P, "xin", [128, D], F32, 2)
    hring = Ring(P, "hbf", [128, D], BF16, 2)
    sring = Ring(P, "nst", [128, 4], F32, 2)
    junk = None

    for kc in range(KC):
        P.dma("sp", lambda e, kc=kc: e.dma_start(out=xt16[:, kc, :], in_=oT_in[kc * 128:(kc + 1) * 128, :]), writes=[xtb])
    pre, epi = epi_resid(env, x, x1, rring, oring)
    groups = [(c0, 512, (lambda ps, psb, blk, ctx, c0=c0: epi(ps, psb, blk, c0, ctx)),
               (lambda blk, c0=c0: pre(blk, c0, 512))) for c0 in range(0, D, 512)]
    linear(env, xt16, xtb, nblk, KC, wo, groups, stage, wbr)
    P.fence("sp")
    build_xt(env, xt16, xtb, make_norm_loader(env, x1, g_tile, xring, hring, sring, junk), nblk, KC, env.ident)
    er = epi_relu2(env, a_s, tring, o16)
    groups = [(c0, 512, (lambda ps, psb, blk, ctx, c0=c0: er(ps, psb, blk, c0)), None) for c0 in range(0, 4 * D, 512)]
    linear(env, xt16, xtb, nblk, KC, w1, groups, stage, wbr, cast_engs=("pool", "dve"))
    P.fence("sp")
    nb2 = TT2 // 128
    xt64 = xt_full[:, :64 * TT2].rearrange("p (k t) -> p k t", k=64)
    for tt in range(NT // TT2):
        tok0 = tt * TT2
        build_xt(env, xt64, xtb, make_bf16_loader(env, a_s, ldring, tok0), nb2, 64, env.ident)
        pre, epi = epi_resid(env, x1, xout, rring, oring, tok0)
        groups = [(c0, 512, (lambda ps, psb, blk, ctx, c0=c0, epi=epi: epi(ps, psb, blk, c0, ctx)),
                   (lambda blk, c0=c0, pre=pre: pre(blk, c0, 512))) for c0 in range(0, D, 512)]
        linear(env, xt64, xtb, nb2, 64, w2, groups, stage, wbr, cast_engs=("pool", "dve"))
    P.finish()
    return nc


def attn_transposes(env, src, sb, dst, dstb, nunits, ident, cnt):
    P = env.P
    for u0 in range(0, nunits, 8):
        nu = min(8, nunits - u0)
        pst, psb = env.banks[4 + (cnt[0] % 2)]
        cnt[0] += 1
        psv = pst[:].bitcast(BF16).rearrange("p (a b) -> p a b", b=128)

        def tr(e, u0=u0, nu=nu, psv=psv):
            ins = None
            for j in range(nu):
                ins = e.transpose(psv[:, j, :], src[:, u0 + j, :], ident[0][:])
            return ins
        P.op("pe", tr, reads=[sb, ident[1]], writes=[psb])
        d = dst[:, u0 * 128:(u0 + nu) * 128].rearrange("p (a b) -> p a b", b=128)
        if cnt[0] % 2 == 0:
            P.op("act", lambda e, d=d, psv=psv, nu=nu: e.copy(d, psv[:, :nu, :]), reads=[psb], writes=[dstb])
        else:
            P.op("dve", lambda e, d=d, psv=psv, nu=nu: e.tensor_copy(d, psv[:, :nu, :]), reads=[psb], writes=[dstb])


def prog_attn_odd(S, NH):
    nc = bass.Bass("TRN2", target_bir_lowering=False)
    env = Env(nc)
    P = env.P
    q = env.inp("q", [S, NH * 128], BF16)
    k = env.inp("k", [S, NH * 128], BF16)
    v = env.inp("v", [S, NH * 128], BF16)
    oT = env.out("oT", [NH * 128, S], BF16)
    common_consts(env)
    mask = env.load_const("mask", [128, 4, 128], F32, cast=BF16)
    ones = P.sbuf("ones", [128, 128], BF16)
    P.op("dve", lambda e: e.memset(ones[0][:], 1.0), writes=[ones[1]])
    NU = S // 128
    scale = 128.0 ** -0.5
    qring = Ring(P, "qtm", [128, NU, 128], BF16, 2)
    kring = Ring(P, "ktm", [128, NU, 128], BF16, 2)
    vring = Ring(P, "vtm", [128, NU, 128], BF16, 2)
    qTring = Ring(P, "qT", [128, S], BF16, 2)
    kTring = Ring(P, "kT", [128, S], BF16, 2)
    accring = Ring(P, "acc", [128, 2, S], F32, 2)
    pring = Ring(P, "pT", [128, 4, 128], BF16, 3)
    pmring = Ring(P, "pTm", [128, 4, 128], BF16, 3)
    lnring = Ring(P, "lnd", [128, S], F32, 1)
    oring = Ring(P, "oTo", [128, S], BF16, 2)
    cnt = [0]
    sbank = 0
    obank = 0
    meng = 0
    for h in range(NH):
        acc, accb = accring.get()
        for bi, dil in enumerate((1, 4, 16)):
            nn = S // (128 * dil)
            qt, qb = qring.get()
            kt, kb = kring.get()
            vt, vb = vring.get()
            for (dst, dstb, srcap) in ((qt, qb, q), (kt, kb, k), (vt, vb, v)):
                sv = srcap[:, h * 128:(h + 1) * 128].rearrange("(n j r) d -> j n r d", j=128, r=dil)
                dv = dst[:].rearrange("p (n r) d -> p n r d", r=dil)
                for r_ in range(dil):
                    P.dma("sp", lambda e, dv=dv, sv=sv, r_=r_: e.dma_start(out=dv[:, :, r_, :], in_=sv[:, :, r_, :]),
                          writes=[dstb])
            qT, qTb = qTring.get()
            kT, kTb = kTring.get()
            attn_transposes(env, qt, qb, qT, qTb, NU, env.ident, cnt)
            attn_transposes(env, kt, kb, kT, kTb, NU, env.ident, cnt)
            units = [(n, r) for n in range(nn) for r in range(dil)]
            for p0 in range(0, NU, 2):
                pair = units[p0:p0 + 2]
                st_, stb = env.banks[sbank % 2]
                sbank += 1
                s4 = st_[:].rearrange("p (a b) -> p a b", b=128)

                def smm(e, pair=pair, s4=s4, qT=qT, kT=kT, dil=dil):
                    ins = None
                    for ui, (n, r) in enumerate(pair):
                        u = n * dil + r
                        qs = qT[:, u * 128:(u + 1) * 128]
                        up = (n - 1) * dil + r if n > 0 else u
                        ins = e.matmul(s4[:, 2 * ui, :], lhsT=kT[:, up * 128:(up + 1) * 128], rhs=qs,
                                       start=True, stop=True)
                        ins = e.matmul(s4[:, 2 * ui + 1, :], lhsT=kT[:, u * 128:(u + 1) * 128], rhs=qs,
                                       start=True, stop=True)
                    return ins
                P.op("pe", smm, reads=[qTb, kTb], writes=[stb])
                pt, ptb = pring.get()
                P.op("act", lambda e, pt=pt, s4=s4: e.activation(out=pt[:], in_=s4, func=AF.Exp, scale=scale),
                     reads=[stb], writes=[ptb])
                pm, pmb = pmring.get()
                me = ("pool", "dve")[meng % 2]
                meng += 1
                P.op(me, lambda e, pm=pm, pt=pt: e.tensor_tensor(out=pm[:], in0=pt[:], in1=mask[0][:], op=ALU.mult),
                     reads=[ptb, mask[1]], writes=[pmb])
                ot_, otb = env.banks[2 + obank % 2]
                obank += 1
                o4 = ot_[:].rearrange("p (u c b) -> p u c b", u=2, c=2)

                def pv(e, pair=pair, o4=o4, pm=pm, vt=vt, dil=dil):
                    ins = None
                    first = True
                    for ui, (n, r) in enumerate(pair):
                        u = n * dil + r
                        for c in range(2):
                            ins = e.matmul(o4[:, ui, c, :], lhsT=(vt[:, u, :] if c == 0 else ones[0][:]),
                                           rhs=pm[:, 2 * ui + 1, :], start=first, stop=(n == 0), skip_group_check=True)
                            first = False
                    for ui, (n, r) in enumerate(pair):
                        if n == 0:
                            continue
                        up = (n - 1) * dil + r
                        for c in range(2):
                            ins = e.matmul(o4[:, ui, c, :], lhsT=(vt[:, up, :] if c == 0 else ones[0][:]),
                                           rhs=pm[:, 2 * ui, :], start=False, stop=True, skip_group_check=True)
                    return ins
                P.op("pe", pv, reads=[pmb, vb, ones[1]], writes=[otb])
                for ui, (n, r) in enumerate(pair):
                    t0 = n * 128 * dil + r
                    dsl = acc[:, :, t0:t0 + 127 * dil + 1:dil]
                    if bi == 0:
                        P.op("dve", lambda e, dsl=dsl, o4=o4, ui=ui: e.tensor_copy(dsl, o4[:, ui, :, :]),
                             reads=[otb], writes=[accb])
                    else:
                        P.op("dve", lambda e, dsl=dsl, o4=o4, ui=ui: e.tensor_tensor(
                            out=dsl, in0=dsl, in1=o4[:, ui, :, :], op=ALU.add), reads=[otb, accb], writes=[accb])
        ln, lnb = lnring.get()
        P.op("act", lambda e, ln=ln, acc=acc: e.activation(out=ln[:], in_=acc[:, 1, :], func=AF.Ln),
             reads=[accb], writes=[lnb])
        P.op("act", lambda e, ln=ln: e.activation(out=ln[:], in_=ln[:], func=AF.Exp, scale=-1.0),
             reads=[lnb], writes=[lnb])
        oo, oob = oring.get()
        P.op("dve", lambda e, oo=oo, acc=acc, ln=ln: e.tensor_tensor(out=oo[:], in0=acc[:, 0, :], in1=ln[:], op=ALU.mult),
             reads=[accb, lnb], writes=[oob])
        P.dma("sp", lambda e, oo=oo, h=h: e.dma_start(out=oT[h * 128:(h + 1) * 128, :], in_=oo[:]), reads=[oob])
    P.finish()
    return nc


def load_tm(env, dst, dstb, src_ap, c0, ncol, nsplit=4):
    P = env.P
    sv = src_ap[:, c0:c0 + ncol].rearrange("(n j) d -> j n d", j=128)
    nb = sv.shape[1]
    nsplit = min(nsplit, nb)
    per = nb // nsplit
    for s_ in range(nsplit):
        P.dma("sp", lambda e, s_=s_: e.dma_start(out=dst[:, s_ * per:(s_ + 1) * per, :],
                                                in_=sv[:, s_ * per:(s_ + 1) * per, :]), writes=[dstb])


def diff_attention(env, S, NHA, qa, ka, va, oT, consts):
    P = env.P
    NB = S // 128
    QT = 512
    scale = 64.0 ** -0.5
    tri, ones, ones32, neglam, gs = consts
    qring = Ring(P, "aq", [128, NB, 128], BF16, 2)
    kring = Ring(P, "ak", [128, NB, 128], BF16, 2)
    vring = Ring(P, "av", [128, NB, 128], BF16, 2)
    qTring = Ring(P, "aqT", [128, S], BF16, 2)
    kTring = Ring(P, "akT", [128, S], BF16, 2)
    pring = Ring(P, "apT", [128, 2, QT], BF16, 3)
    rring = Ring(P, "arinv", [128, 2, QT], F32, 2)
    tring = Ring(P, "at", [128, 2, QT], F32, 2)
    oring = Ring(P, "aoT", [128, QT], BF16, 2)
    cnt = [0]
    meng = 0
    sbank = 0
    for h in range(NHA):
        qt, qb = qring.get()
        kt, kb = kring.get()
        vt, vb = vring.get()
        load_tm(env, qt, qb, qa, h * 128, 128)
        load_tm(env, kt, kb, ka, h * 128, 128)
        load_tm(env, vt, vb, va, h * 128, 128)
        qT, qTb = qTring.get()
        kT, kTb = kTring.get()
        attn_transposes(env, qt, qb, qT, qTb, NB, env.ident, cnt)
        attn_transposes(env, kt, kb, kT, kTb, NB, env.ident, cnt)
        for qi in range(S // QT):
            q0 = qi * QT
            nkb = (q0 + QT) // 128
            accs = [env.banks[0], env.banks[1], env.banks[2], env.banks[3]]
            for kbi in range(nkb):
                col0 = max(0, kbi * 128 - q0)
                w = QT - col0
                sb0 = env.banks[6 + sbank % 2]
                sbank += 1
                sb1 = env.banks[6 + sbank % 2]
                sbank += 1
                sts = (sb0, sb1)

                def smm(e, kbi=kbi, col0=col0, q0=q0, sts=sts, qT=qT, kT=kT):
                    ins = None
                    for c in range(2):
                        ins = e.matmul(sts[c][0][:, col0:QT],
                                       lhsT=kT[c * 64:(c + 1) * 64, kbi * 128:(kbi + 1) * 128],
                                       rhs=qT[c * 64:(c + 1) * 64, q0 + col0:q0 + QT], start=True, stop=True)
                    return ins
                P.op("pe", smm, reads=[qTb, kTb], writes=[sts[0][1], sts[1][1]])
                pt, ptb = pring.get()
                for c in range(2):
                    P.op("act", lambda e, pt=pt, c=c, col0=col0, sts=sts: e.activation(
                        out=pt[:, c, col0:QT], in_=sts[c][0][:, col0:QT], func=AF.Exp, scale=scale),
                        reads=[sts[c][1]], writes=[ptb])
                if kbi * 128 >= q0:
                    me = ("pool", "dve")[meng % 2]
                    meng += 1
                    P.op(me, lambda e, pt=pt, col0=col0: e.tensor_tensor(
                        out=pt[:, :, col0:col0 + 128], in0=pt[:, :, col0:col0 + 128], in1=tri[0][:], op=ALU.mult),
                        reads=[ptb, tri[1]], writes=[ptb])

                def pv(e, kbi=kbi, col0=col0, pt=pt, vt=vt, accs=accs, nkb=nkb):
                    ins = None
                    for c in range(2):
                        ins = e.matmul(accs[c][0][:, col0:QT], lhsT=vt[:, kbi, :], rhs=pt[:, c, col0:QT],
                                       start=(kbi == 0), stop=(kbi == nkb - 1))
                        ins = e.matmul(accs[2 + c][0][:, col0:QT], lhsT=ones[0][:], rhs=pt[:, c, col0:QT],
                                       start=(kbi == 0), stop=(kbi == nkb - 1))
                    return ins
                P.op("pe", pv, reads=[ptb, vb, ones[1]], writes=[a[1] for a in accs], acc=(kbi > 0))
            rv, rvb = rring.get()
            for c in range(2):
                P.op("act", lambda e, rv=rv, c=c, accs=accs: e.activation(out=rv[:, c, :], in_=accs[2 + c][0][:], func=AF.Ln),
                     reads=[accs[2 + c][1]], writes=[rvb])
            P.op("act", lambda e, rv=rv: e.activation(out=rv[:], in_=rv[:], func=AF.Exp, scale=-1.0),
                 reads=[rvb], writes=[rvb])
            tt, ttb = tring.get()
            for c in range(2):
                P.op("dve", lambda e, tt=tt, c=c, rv=rv, accs=accs: e.tensor_tensor(
                    out=tt[:, c, :], in0=accs[c][0][:], in1=rv[:, c, :], op=ALU.mult),
                    reads=[accs[c][1], rvb], writes=[ttb])
            P.op("dve", lambda e, tt=tt: e.scalar_tensor_tensor(out=tt[:, 0, :], in0=tt[:, 1, :], scalar=neglam[0][:, 0:1],
                                                               in1=tt[:, 0, :], op0=ALU.mult, op1=ALU.add),
                 reads=[ttb, neglam[1]], writes=[ttb])
            P.op("act", lambda e, tt=tt: e.activation(out=tt[:, 1, :], in_=tt[:, 0, :], func=AF.Square),
                 reads=[ttb], writes=[ttb])
            ssb = env.banks[6 + sbank % 2]
            sbank += 1
            P.op("pe", lambda e, ssb=ssb, tt=tt: e.matmul(ssb[0][:], lhsT=ones32[0][:], rhs=tt[:, 1, :], start=True, stop=True),
                 reads=[ttb, ones32[1]], writes=[ssb[1]])
            P.op("act", lambda e, tt=tt, ssb=ssb: e.activation(out=tt[:, 1, :], in_=ssb[0][:], func=AF.Ln,
                                                              scale=1.0 / 128, bias=env.eps[0][:, 0:1]),
                 reads=[ssb[1], env.eps[1]], writes=[ttb])
            P.op("act", lambda e, tt=tt: e.activation(out=tt[:, 1, :], in_=tt[:, 1, :], func=AF.Exp, scale=-0.5),
                 reads=[ttb], writes=[ttb])
            oo, oob = oring.get()
            P.op("dve", lambda e, oo=oo, tt=tt: e.scalar_tensor_tensor(out=oo[:], in0=tt[:, 0, :], scalar=gs[0][:, 0:1],
                                                                      in1=tt[:, 1, :], op0=ALU.mult, op1=ALU.mult),
                 reads=[ttb, gs[1]], writes=[oob])
            P.dma("sp", lambda e, oo=oo, h=h, q0=q0: e.dma_start(out=oT[h * 128:(h + 1) * 128, q0:q0 + QT], in_=oo[:]),
                  reads=[oob])


def attn_even_consts(env):
    P = env.P
    tri = env.load_const("tri", [128, 2, 128], F32, cast=BF16)
    ones = P.sbuf("ones", [128, 128], BF16)
    P.op("dve", lambda e: e.memset(ones[0][:], 1.0), writes=[ones[1]])
    ones32 = P.sbuf("ones32", [128, 128], F32)
    P.op("dve", lambda e: e.memset(ones32[0][:], 1.0), writes=[ones32[1]])
    lamv = env.load_const("lamv", [128, 4, 64])
    lam0 = env.load_const("lam0", [128, 2])
    gsub = env.load_const("gsub", [128, 1])
    pr, prb = P.sbuf("lampr", [128, 2, 64], F32)
    sm, smb = P.sbuf("lamsm", [128, 4], F32)
    P.op("dve", lambda e: e.tensor_tensor(out=pr[:, 0, :], in0=lamv[0][:, 0, :], in1=lamv[0][:, 1, :], op=ALU.mult),
         reads=[lamv[1]], writes=[prb])
    P.op("dve", lambda e: e.tensor_tensor(out=pr[:, 1, :], in0=lamv[0][:, 2, :], in1=lamv[0][:, 3, :], op=ALU.mult),
         reads=[lamv[1]], writes=[prb])
    P.op("dve", lambda e: e.tensor_reduce(out=sm[:, 0:2], in_=pr[:], axis=AX.X, op=ALU.add), reads=[prb], writes=[smb])
    P.op("act", lambda e: e.activation(out=sm[:, 0:2], in_=sm[:, 0:2], func=AF.Exp), reads=[smb], writes=[smb])
    P.op("dve", lambda e: e.tensor_tensor(out=sm[:, 2:3], in0=sm[:, 1:2], in1=sm[:, 0:1], op=ALU.subtract),
         reads=[smb], writes=[smb])
    neglam = P.sbuf("neglam", [128, 1], F32)
    P.op("dve", lambda e: e.tensor_tensor(out=neglam[0][:], in0=sm[:, 2:3], in1=lam0[0][:, 0:1], op=ALU.subtract),
         reads=[smb, lam0[1]], writes=[neglam[1]])
    gs = P.sbuf("gs", [128, 1], F32)
    P.op("dve", lambda e: e.tensor_tensor(out=gs[0][:], in0=gsub[0][:], in1=lam0[0][:, 1:2], op=ALU.mult),
         reads=[gsub[1], lam0[1]], writes=[gs[1]])
    return tri, ones, ones32, neglam, gs


import os as _os
_DBG_STOP = int(_os.environ.get('DBG_STOP', '0'))


def gdn(env, S, row0, NH, h0, aps, oT):
    P = env.P
    NB = S // 128
    W = NH * 128
    cs_ = slice(h0 * 128, (h0 + NH) * 128)
    bq = aps["bq"][:, cs_]
    bk = aps["bk"][:, cs_]
    bv = aps["bv"][:, cs_]
    bz = aps["bz"][:, cs_]
    NHT = aps["bab"].shape[1] // 2
    ba_ap = aps["bab"][:, h0:h0 + NH]
    bb_ap = aps["bab"][:, NHT + h0:NHT + h0 + NH]
    cw = env.load_ap("cw", aps["cw"][:, :, :, cs_], [128, 3, 4, W])
    alog = env.load_ap("alog", aps["alog"][:, h0:h0 + NH], [128, NH])
    dtb = env.load_ap("dtb", aps["dtb"][:, h0:h0 + NH], [128, NH])
    gout = env.load_ap("gout", aps["gout"], [128, 128])
    mU = env.load_ap("mU", aps["mU"], [128, 128])
    mC = env.load_ap("mC", aps["mC"], [128, 128])
    mB0 = env.load_ap("mB0", aps["mB0"], [128, 128])
    mB1 = env.load_ap("mB1", aps["mB1"], [128, 128])
    MBL = env.load_ap("MBL", aps["MBL"][:, 0:NH, :], [128, NH, 128])
    MBU = env.load_ap("MBU", aps["MBU"][:, 0:NH, :], [128, NH, 128])
    NOTI = env.load_ap("NOTI", aps["NOTI"][:, 0:NH, :], [128, NH, 128])
    id32 = env.load_ap("id32", aps["id32"], [128, 128])
    ones32 = P.sbuf("gones32", [128, 128], F32)
    P.op("dve", lambda e: e.memset(ones32[0][:], 1.0), writes=[ones32[1]])
    nones32 = P.sbuf("gnones32", [128, 128], F32)
    P.op("dve", lambda e: e.memset(nones32[0][:], -1.0), writes=[nones32[1]])
    one1 = P.sbuf("one1", [128, 1], F32)
    P.op("dve", lambda e: e.memset(one1[0][:], 1.0), writes=[one1[1]])
    nea = P.sbuf("nea", [128, NH], F32)
    P.op("act", lambda e: e.activation(out=nea[0][:], in_=alog[0][:], func=AF.Exp), reads=[alog[1]], writes=[nea[1]])
    P.op("dve", lambda e: e.tensor_scalar(out=nea[0][:], in0=nea[0][:], scalar1=-1.0, scalar2=None, op0=ALU.mult),
         reads=[nea[1]], writes=[nea[1]])
    states = [P.sbuf("gS%d" % h, [128, 128], F32) for h in range(NH)]
    for h in range(NH):
        P.op("dve", lambda e, h=h: e.memset(states[h][0][:], 0.0), writes=[states[h][1]])

    xring = Ring(P, "gx", [128, 4, W], F32, 6)
    mring = Ring(P, "gm", [128, 4, W], F32, 2)
    cring = Ring(P, "gc", [128, 3, W], F32, 2)
    zring = Ring(P, "gz", [128, W], F32, 2)
    kdring = Ring(P, "gkd", [128, W], F32, 2)
    abring = Ring(P, "gab", [128, 2 * NH], F32, 2)
    smring = Ring(P, "gsm", [128, 64], F32, 2)
    tkring = Ring(P, "gtk", [128, 6, W], F32, 2)
    ugring = Ring(P, "gug", [128, NH, 128], F32, 2)
    dring = Ring(P, "gd", [128, 3, NH, 128], F32, 2)
    trring = Ring(P, "gtr", [128, 4, 128], F32, 2 * NH)
    aaring = Ring(P, "gaa", [128, 128], F32, 2 * NH)
    xrring = Ring(P, "gxr", [128, 2, 128], F32, 3 * NH)
    ajring = Ring(P, "gaj", [128, 128], F32, 3 * NH)
    uwring = Ring(P, "guw", [128, 2, 128], F32, 2 * NH)
    vnring = Ring(P, "gvn", [128, 128], F32, 4 * NH)
    otring = Ring(P, "got", [128, NH, 128], F32, 2)
    o2ring = Ring(P, "go2", [128, 2, W], F32, 2)
    obring = Ring(P, "gob", [128, W], BF16, 2)
    oTring = Ring(P, "goT", [128, NH, 512], BF16, 2)
    wbank = [0]
    if _os.environ.get("DBG_DUMP") == "2":
        env._dbt = P.sbuf("dbgp", [128, 3, 128], F32)

    def bank():
        b = env.banks[wbank[0] % 4]
        wbank[0] += 1
        return b

    def bc(ap, n):
        return ap.unsqueeze(2).broadcast_to([128, NH, n])

    def block_body(blk, oTt):
        t0 = blk * 128
        xs = []
        for ti, src in enumerate((bq, bk, bv)):
            xt_, xb = xring.get()
            if blk == 0:
                P.op("pool", lambda e, xt_=xt_: e.memset(xt_[:], 0.0), writes=[xb])
            for sft in range(4):
                lo = t0 - sft
                p0 = 0
                if lo < 0:
                    p0 = -lo
                    lo = 0
                P.dma("sp", lambda e, xt_=xt_, sft=sft, lo=lo, p0=p0, src=src: e.dma_start(
                    out=xt_[p0:128, sft, :], in_=src[lo:lo + 128 - p0, :]), writes=[xb])
            xs.append((xt_, xb))
        zt, zb = zring.get()
        P.dma("sp", lambda e, zt=zt: e.dma_start(out=zt[:], in_=bz[t0:t0 + 128, :]), writes=[zb])
        ab, abb = abring.get()
        P.dma("sp", lambda e, ab=ab: e.dma_start(out=ab[:, 0:NH], in_=ba_ap[t0:t0 + 128, :], allow_slow_non_contiguous=True), writes=[abb])
        P.dma("sp", lambda e, ab=ab: e.dma_start(out=ab[:, NH:2 * NH], in_=bb_ap[t0:t0 + 128, :], allow_slow_non_contiguous=True), writes=[abb])
        cv, cvb = cring.get()
        for ti in range(3):
            xt_, xb = xs[ti]
            m, mb = mring.get()
            P.op("pool", lambda e, m=m, xt_=xt_, ti=ti: e.tensor_tensor(
                out=m[:], in0=xt_[:], in1=cw[0][:, ti, :, :], op=ALU.mult),
                reads=[xb, cw[1]], writes=[mb])
            P.op("dve", lambda e, m=m: e.tensor_tensor(out=m[:, 0:2, :], in0=m[:, 0:2, :], in1=m[:, 2:4, :], op=ALU.add),
                 reads=[mb], writes=[mb])
            P.op("dve", lambda e, m=m, cv=cv, ti=ti: e.tensor_tensor(out=cv[:, ti, :], in0=m[:, 0, :], in1=m[:, 1, :], op=ALU.add),
                 reads=[mb], writes=[cvb])
        P.op("act", lambda e, cv=cv: e.activation(out=cv[:], in_=cv[:], func=AF.Silu), reads=[cvb], writes=[cvb])
        env.dump("cv", cv[:], cvb)
        if _DBG_STOP == 1:
            return oTt
        sm, smb = smring.get()
        m, mb = mring.get()
        P.op("act", lambda e, m=m, cv=cv: e.activation(out=m[:, 0:2, :], in_=cv[:, 0:2, :], func=AF.Square),
             reads=[cvb], writes=[mb])
        P.op("dve", lambda e, m=m, sm=sm: e.tensor_reduce(
            out=sm[:, 0:2 * NH], in_=m[:, 0:2, :].rearrange("p t (h d) -> p (t h) d", d=128), axis=AX.X, op=ALU.add),
            reads=[mb], writes=[smb])
        P.op("act", lambda e, sm=sm: e.activation(out=sm[:, 0:2 * NH], in_=sm[:, 0:2 * NH], func=AF.Ln,
                                                 bias=env.eps[0][:, 0:1]), reads=[smb, env.eps[1]], writes=[smb])
        P.op("act", lambda e, sm=sm: e.activation(out=sm[:, 0:2 * NH], in_=sm[:, 0:2 * NH], func=AF.Exp, scale=-0.5),
             reads=[smb], writes=[smb])
        P.op("dve", lambda e, sm=sm: e.tensor_scalar(out=sm[:, 0:NH], in0=sm[:, 0:NH], scalar1=128.0 ** -0.5,
                                                    scalar2=None, op0=ALU.mult), reads=[smb], writes=[smb])
        GB = 2 * NH
        P.op("act", lambda e, sm=sm, ab=ab: e.activation(out=sm[:, GB:GB + NH], in_=ab[:, NH:2 * NH], func=AF.Sigmoid),
             reads=[abb], writes=[smb])
        P.op("dve", lambda e, sm=sm, ab=ab: e.tensor_tensor(out=sm[:, GB + NH:GB + 2 * NH], in0=ab[:, 0:NH], in1=dtb[0][:], op=ALU.add),
             reads=[abb, dtb[1]], writes=[smb])
        P.op("act", lambda e, sm=sm: e.activation(out=sm[:, GB + NH:GB + 2 * NH], in_=sm[:, GB + NH:GB + 2 * NH], func=AF.Exp),
             reads=[smb], writes=[smb])
        P.op("act", lambda e, sm=sm: e.activation(out=sm[:, GB + NH:GB + 2 * NH], in_=sm[:, GB + NH:GB + 2 * NH], func=AF.Ln,
                                                 bias=one1[0][:, 0:1]), reads=[smb, one1[1]], writes=[smb])
        P.op("dve", lambda e, sm=sm: e.tensor_tensor(out=sm[:, GB + NH:GB + 2 * NH], in0=sm[:, GB + NH:GB + 2 * NH], in1=nea[0][:], op=ALU.mult),
             reads=[smb, nea[1]], writes=[smb])
        beta = sm[:, GB:GB + NH]
        gg = sm[:, GB + NH:GB + 2 * NH]
        EB = 4 * NH
        for mi, mm_ in enumerate((mU, mC, mB0, mB1)):
            cb = bank()
            P.op("pe", lambda e, cb=cb, gg=gg, mm_=mm_: e.matmul(cb[0][:, 0:NH], lhsT=mm_[0][:], rhs=gg, start=True, stop=True),
                 reads=[smb, mm_[1]], writes=[cb[1]])
            P.op("dve", lambda e, sm=sm, cb=cb, mi=mi: e.tensor_copy(sm[:, EB + mi * NH:EB + (mi + 1) * NH], cb[0][:, 0:NH]),
                 reads=[cb[1]], writes=[smb])
        P.op("dve", lambda e, sm=sm: e.tensor_tensor(out=sm[:, EB + NH:EB + 2 * NH], in0=sm[:, EB + NH:EB + 2 * NH],
                                                    in1=sm[:, EB:EB + NH], op=ALU.subtract), reads=[smb], writes=[smb])
        P.op("dve", lambda e, sm=sm: e.tensor_copy(sm[:, EB + 4 * NH:EB + 5 * NH], sm[:, EB:EB + NH]), reads=[smb], writes=[smb])
        P.op("act", lambda e, sm=sm: e.activation(out=sm[:, EB:EB + 4 * NH], in_=sm[:, EB:EB + 4 * NH], func=AF.Exp),
             reads=[smb], writes=[smb])
        egc = sm[:, EB:EB + NH]
        ekd = sm[:, EB + NH:EB + 2 * NH]
        env.dump("sm", sm[:, 0:9 * NH], smb)
        if _DBG_STOP == 2:
            return oTt
        tk, tkb = tkring.get()
        c3 = lambda a: a.rearrange("p (h d) -> p h d", d=128)
        Kt, Qs, kbt, kbg, vbt, qd, kdec = 0, 1, 2, 3, 4, 5, None
        P.op("dve", lambda e, tk=tk, cv=cv, sm=sm: e.tensor_tensor(out=c3(tk[:, 0, :]), in0=c3(cv[:, 1, :]), in1=bc(sm[:, NH:2 * NH], 128), op=ALU.mult),
             reads=[cvb, smb], writes=[tkb])
        P.op("dve", lambda e, tk=tk, cv=cv, sm=sm: e.tensor_tensor(out=c3(tk[:, 1, :]), in0=c3(cv[:, 0, :]), in1=bc(sm[:, 0:NH], 128), op=ALU.mult),
             reads=[cvb, smb], writes=[tkb])
        P.op("pool", lambda e, tk=tk: e.tensor_tensor(out=c3(tk[:, 2, :]), in0=c3(tk[:, 0, :]), in1=bc(beta, 128), op=ALU.mult),
             reads=[tkb, smb], writes=[tkb])
        P.op("pool", lambda e, tk=tk: e.tensor_tensor(out=c3(tk[:, 3, :]), in0=c3(tk[:, 2, :]), in1=bc(egc, 128), op=ALU.mult),
             reads=[tkb, smb], writes=[tkb])
        P.op("pool", lambda e, tk=tk, cv=cv: e.tensor_tensor(out=c3(tk[:, 4, :]), in0=c3(cv[:, 2, :]), in1=bc(beta, 128), op=ALU.mult),
             reads=[tkb, cvb, smb], writes=[tkb])
        P.op("dve", lambda e, tk=tk: e.tensor_tensor(out=c3(tk[:, 5, :]), in0=c3(tk[:, 1, :]), in1=bc(egc, 128), op=ALU.mult),
             reads=[tkb, smb], writes=[tkb])
        kd, kdb = kdring.get()
        P.op("pool", lambda e, kd=kd, tk=tk: e.tensor_tensor(out=c3(kd[:]), in0=c3(tk[:, 0, :]), in1=bc(ekd, 128), op=ALU.mult),
             reads=[tkb, smb], writes=[kdb])
        env.dump("tk", tk[:], tkb)
        env.dump("kd", kd[:], kdb)
        if _DBG_STOP == 3:
            return oTt
        ug, ugb = ugring.get()
        P.op("dve", lambda e, ug=ug: e.tensor_tensor(out=ug[:], in0=mU[0][:].unsqueeze(1).broadcast_to([128, NH, 128]),
                                                    in1=bc(gg, 128), op=ALU.mult), reads=[mU[1], smb], writes=[ugb])
        dd, ddb = dring.get()
        gcs = sm[:, EB + 4 * NH:EB + 5 * NH]
        for h in range(NH):
            db = bank()
            P.op("pe", lambda e, db=db, ug=ug, h=h: e.matmul(db[0][:, 0:128], lhsT=ones32[0][:], rhs=ug[:, h, :], start=True, stop=True),
                 reads=[ugb, ones32[1]], writes=[db[1]])
            P.op("dve", lambda e, dd=dd, db=db, gcs=gcs, h=h: e.tensor_tensor(
                out=dd[:, 2, h, :], in0=gcs[:, h:h + 1].broadcast_to([128, 128]), in1=db[0][:, 0:128], op=ALU.subtract),
                reads=[db[1], smb], writes=[ddb])
        P.op("pool", lambda e, dd=dd: e.tensor_tensor(out=dd[:, 0, :, :], in0=dd[:, 2, :, :], in1=MBL[0][:], op=ALU.add),
             reads=[ddb, MBL[1]], writes=[ddb])
        P.op("dve", lambda e, dd=dd: e.scalar_tensor_tensor(out=dd[:, 1, :, :], in0=dd[:, 2, :, :], scalar=-1.0, in1=MBU[0][:],
                                                           op0=ALU.mult, op1=ALU.add), reads=[ddb, MBU[1]], writes=[ddb])
        P.op("act", lambda e, dd=dd: e.activation(out=dd[:, 0:2, :, :], in_=dd[:, 0:2, :, :], func=AF.Exp), reads=[ddb], writes=[ddb])
        P.op("pool", lambda e, dd=dd: e.tensor_tensor(out=dd[:, 2, :, :], in0=dd[:, 1, :, :], in1=NOTI[0][:], op=ALU.mult),
             reads=[ddb, NOTI[1]], writes=[ddb])
        env.dump("dd", dd[:], ddb)
        if _DBG_STOP == 4:
            return oTt
        heads = []
        for h in range(NH):
            tb_ = bank()
            t4 = tb_[0][:].rearrange("p (a b) -> p a b", b=128)
            def trs(e, t4=t4, tk=tk, h=h):
                ins = None
                for si, srci in enumerate((0, 2, 1, 5)):
                    ins = e.transpose(t4[:, si, :], tk[:, srci, h * 128:(h + 1) * 128], id32[0][:])
                return ins
            P.op("pe", trs, reads=[tkb, id32[1]], writes=[tb_[1]])
            tr, trb = trring.get()
            P.op("act", lambda e, tr=tr, t4=t4: e.copy(tr[:], t4), reads=[tb_[1]], writes=[trb])
            gb_ = bank()
            g4 = gb_[0][:].rearrange("p (a b) -> p a b", b=128)
            gb2 = bank()
            g42 = gb2[0][:].rearrange("p (a b) -> p a b", b=128)
            P.op("pe", lambda e, g4=g4, tr=tr: e.matmul(g4[:, 0:2, :], lhsT=tr[:, 0, :], rhs=tr[:, 1:3, :], start=True, stop=True),
                 reads=[trb], writes=[gb_[1]])
            P.op("pe", lambda e, g42=g42, tr=tr: e.matmul(g42[:, 2, :], lhsT=tr[:, 1, :], rhs=tr[:, 0, :], start=True, stop=True),
                 reads=[trb], writes=[gb2[1]])
            xr, xrb = xrring.get()
            aj, ajb = ajring.get()
            aa, aab = aaring.get()
            P.op("dve", lambda e, xr=xr, g4=g4, dd=dd, h=h: e.scalar_tensor_tensor(
                out=xr[:, 0, :], in0=g4[:, 0, :], scalar=-1.0, in1=dd[:, 2, h, :], op0=ALU.mult, op1=ALU.mult),
                reads=[gb_[1], ddb], writes=[xrb])
            P.op("dve", lambda e, aa=aa, g4=g4, dd=dd, h=h: e.tensor_tensor(out=aa[:], in0=g4[:, 1, :], in1=dd[:, 1, h, :], op=ALU.mult),
                 reads=[gb_[1], ddb], writes=[aab])
            P.op("dve", lambda e, aj=aj, g42=g42, dd=dd, h=h: e.scalar_tensor_tensor(
                out=aj[:], in0=g42[:, 2, :], scalar=-1.0, in1=dd[:, 0, h, :], op0=ALU.mult, op1=ALU.mult),
                reads=[gb2[1], ddb], writes=[ajb])
            P.op("pool", lambda e, xr=xr: e.tensor_tensor(out=xr[:, 1, :], in0=xr[:, 0, :], in1=id32[0][:], op=ALU.add),
                 reads=[xrb, id32[1]], writes=[xrb])
            heads.append(dict(tr=tr, trb=trb, xr=xr, xrb=xrb, aj=aj, ajb=ajb, aa=aa, aab=aab))
            env.dump("tr", tr[:], trb)
            env.dump("aa", aa[:], aab)
            env.dump("xr0", xr[:], xrb)
            env.dump("aj0", aj[:], ajb)
        if _DBG_STOP == 5:
            return oTt
        for lvl in range(6):
            for h in range(NH):
                H = heads[h]
                xr, xrb, aj, ajb = H["xr"], H["xrb"], H["aj"], H["ajb"]
                sb_ = bank()
                s4 = sb_[0][:].rearrange("p (a b) -> p a b", b=128)
                def lv(e, s4=s4, xr=xr, aj=aj, lvl=lvl):
                    if lvl == 0:
                        return e.matmul(s4[:, 0, :], lhsT=aj[:], rhs=xr[:, 0, :], start=True, stop=True)
                    elif lvl <= 3:
                        return e.matmul(s4[:, 0:2, :], lhsT=aj[:], rhs=xr[:, 0:2, :], start=True, stop=True)
                    return e.matmul(s4[:, 1, :], lhsT=aj[:], rhs=xr[:, 1, :], start=True, stop=True)
                P.op("pe", lv, reads=[xrb, ajb], writes=[sb_[1]])
                if lvl <= 4:
                    sb2 = bank()
                    s42 = sb2[0][:].rearrange("p (a b) -> p a b", b=128)
                    P.op("pe", lambda e, s42=s42, xr=xr, aj=aj: e.matmul(s42[:, 2, :], lhsT=xr[:, 0, :], rhs=aj[:], start=True, stop=True),
                         reads=[xrb, ajb], writes=[sb2[1]])
                nxr, nxrb = xrring.get()
                naj, najb = ajring.get()
                if lvl <= 3:
                    P.op("act", lambda e, nxr=nxr, s4=s4: e.copy(nxr[:, 0, :], s4[:, 0, :]), reads=[sb_[1]], writes=[nxrb])
                if lvl == 0:
                    P.op("pool", lambda e, nxr=nxr, xr=xr: e.tensor_copy(nxr[:, 1, :], xr[:, 1, :]), reads=[xrb], writes=[nxrb])
                else:
                    P.op("dve", lambda e, nxr=nxr, xr=xr, s4=s4: e.tensor_tensor(out=nxr[:, 1, :], in0=s4[:, 1, :], in1=xr[:, 1, :], op=ALU.add),
                         reads=[sb_[1], xrb], writes=[nxrb])
                if lvl <= 4:
                    P.op("act", lambda e, naj=naj, s42=s42: e.copy(naj[:], s42[:, 2, :]), reads=[sb2[1]], writes=[najb])
                H["xr"], H["xrb"], H["aj"], H["ajb"] = nxr, nxrb, naj, najb
        if _DBG_STOP == 6:
            return oTt
        for h in range(NH):
            H = heads[h]
            ub_ = bank()
            u4 = ub_[0][:].rearrange("p (a b) -> p a b", b=128)
            ub2 = bank()
            u42 = ub2[0][:].rearrange("p (a b) -> p a b", b=128)
            P.op("pe", lambda e, u4=u4, H=H, tk=tk, h=h: e.matmul(u4[:, 0, :], lhsT=H["xr"][:, 1, :], rhs=tk[:, 4, h * 128:(h + 1) * 128],
                                                              start=True, stop=True), reads=[H["xrb"], tkb], writes=[ub_[1]])
            P.op("pe", lambda e, u42=u42, H=H, tk=tk, h=h: e.matmul(u42[:, 1, :], lhsT=tk[:, 3, h * 128:(h + 1) * 128], rhs=H["xr"][:, 1, :],
                                                                start=True, stop=True), reads=[H["xrb"], tkb], writes=[ub2[1]])
            uw, uwb = uwring.get()
            P.op("act", lambda e, uw=uw, u4=u4: e.copy(uw[:, 0, :], u4[:, 0, :]), reads=[ub_[1]], writes=[uwb])
            P.op("act", lambda e, uw=uw, u42=u42: e.copy(uw[:, 1, :], u42[:, 1, :]), reads=[ub2[1]], writes=[uwb])
            H["uw"], H["uwb"] = uw, uwb
            env.dump("TT", H["xr"][:], H["xrb"])
            env.dump("uw", uw[:], uwb)
            H["vn"] = []
            for c_ in range(2):
                vn_, vnb_ = vnring.get()
                P.op("pool", lambda e, vn_=vn_: e.memset(vn_[:], 0.0), writes=[vnb_])
                H["vn"].append((vn_, vnb_))
        if _DBG_STOP == 7:
            return oTt
        ot, otb = otring.get()
        for c in range(2):
            r0 = 64 * c
            for h in range(NH):
                H = heads[h]
                St, Sb = states[h]
                pb_ = env.banks[4 + h % 2]
                p4 = pb_[0][:].rearrange("p (a b) -> p a b", b=128)
                qb_ = env.banks[6 + h % 2]
                q4 = qb_[0][:].rearrange("p (a b) -> p a b", b=128)
                uw = H["uw"]
                vn, vnb = H["vn"][c]
                P.op("pe", lambda e, p4=p4, uw=uw, St=St: e.matmul(p4[:, 0, :], lhsT=uw[:, 1, :], rhs=St[:], start=True, stop=True),
                     reads=[H["uwb"], Sb], writes=[pb_[1]])
                P.op("dve", lambda e, vn=vn, uw=uw, p4=p4, r0=r0: e.tensor_tensor(
                    out=vn[r0:r0 + 64, :], in0=uw[r0:r0 + 64, 0, :], in1=p4[r0:r0 + 64, 0, :], op=ALU.subtract),
                    reads=[H["uwb"], pb_[1]], writes=[vnb])
                def s2(e, q4=q4, H=H, St=St, vn=vn):
                    e.matmul(q4[:, 1, :], lhsT=H["tr"][:, 3, :], rhs=St[:], start=True, stop=False)
                    return e.matmul(q4[:, 1, :], lhsT=H["aa"][:], rhs=vn[:], start=False, stop=True)
                P.op("pe", s2, reads=[H["trb"], Sb, H["aab"], vnb], writes=[qb_[1]])
                P.op("pe", lambda e, p4=p4, vn=vn, h=h, kd=kd: e.matmul(p4[:, 2, :], lhsT=kd[:, h * 128:(h + 1) * 128], rhs=vn[:],
                                                                      start=True, stop=True),
                     reads=[vnb, kdb], writes=[pb_[1]], acc=True)
                P.op("act", lambda e, ot=ot, q4=q4, r0=r0, h=h: e.copy(ot[r0:r0 + 64, h, :], q4[r0:r0 + 64, 1, :]),
                     reads=[qb_[1]], writes=[otb])
                if _os.environ.get("DBG_DUMP") in ("3", "4") and h == 0:
                    env.dump("vn_%d" % c, vn[:], vnb)
                P.op("dve", lambda e, St=St, p4=p4, sm=sm, c=c, h=h: e.scalar_tensor_tensor(
                    out=St[:], in0=St[:], scalar=sm[:, EB + (2 + c) * NH + h:EB + (2 + c) * NH + h + 1], in1=p4[:, 2, :],
                    op0=ALU.mult, op1=ALU.add), reads=[Sb, pb_[1], smb], writes=[Sb])
        if _DBG_STOP == 8:
            return oTt
        o2, o2b = o2ring.get()
        sm2, sm2b = smring.get()
        P.op("act", lambda e, o2=o2, ot=ot: e.activation(out=o2[:, 0, :], in_=ot[:].rearrange("p h d -> p (h d)"), func=AF.Square),
             reads=[otb], writes=[o2b])
        P.op("dve", lambda e, o2=o2, sm2=sm2: e.tensor_reduce(out=sm2[:, 0:NH], in_=c3(o2[:, 0, :]), axis=AX.X, op=ALU.add),
             reads=[o2b], writes=[sm2b])
        P.op("act", lambda e, sm2=sm2: e.activation(out=sm2[:, 0:NH], in_=sm2[:, 0:NH], func=AF.Ln, scale=1.0 / 128,
                                                   bias=env.eps[0][:, 0:1]), reads=[sm2b, env.eps[1]], writes=[sm2b])
        P.op("act", lambda e, sm2=sm2: e.activation(out=sm2[:, 0:NH], in_=sm2[:, 0:NH], func=AF.Exp, scale=-0.5),
             reads=[sm2b], writes=[sm2b])
        P.op("dve", lambda e, o2=o2, ot=ot, sm2=sm2: e.tensor_tensor(out=c3(o2[:, 0, :]), in0=ot[:], in1=bc(sm2[:, 0:NH], 128), op=ALU.mult),
             reads=[otb, sm2b], writes=[o2b])
        P.op("pool", lambda e, o2=o2: e.tensor_tensor(out=c3(o2[:, 0, :]), in0=c3(o2[:, 0, :]),
                                                     in1=gout[0][:].unsqueeze(1).broadcast_to([128, NH, 128]), op=ALU.mult),
             reads=[o2b, gout[1]], writes=[o2b])
        P.op("act", lambda e, o2=o2, zt=zt: e.activation(out=o2[:, 1, :], in_=zt[:], func=AF.Silu), reads=[zb], writes=[o2b])
        ob, obb = obring.get()
        P.op("dve", lambda e, ob=ob, o2=o2: e.tensor_tensor(out=ob[:], in0=o2[:, 0, :], in1=o2[:, 1, :], op=ALU.mult),
             reads=[o2b], writes=[obb])
        env.dump("ot", ot[:], otb)
        env.dump("o2", o2[:], o2b)
        env.dump("S0", states[0][0][:], states[0][1])
        if _DBG_STOP == 9:
            return oTt
        tb_ = bank()
        tv = tb_[0][:].bitcast(BF16).rearrange("p (a b) -> p a b", b=128)
        def otr(e, tv=tv, ob=ob):
            ins = None
            for h in range(NH):
                ins = e.transpose(tv[:, h, :], ob[:, h * 128:(h + 1) * 128], env.ident[0][:])
            return ins
        P.op("pe", otr, reads=[obb, env.ident[1]], writes=[tb_[1]])
        if _DBG_STOP == 10:
            return oTt
        if blk % 4 == 0:
            oTt = oTring.get()
        sl = blk % 4
        P.op("act", lambda e, oTt=oTt, tv=tv, sl=sl: e.copy(oTt[0][:, :, sl * 128:(sl + 1) * 128], tv[:, 0:NH, :]),
             reads=[tb_[1]], writes=[oTt[1]])
        if _DBG_STOP == 11:
            return oTt
        if sl == 3 or blk == NB - 1:
            b0 = (blk // 4) * 512
            nt = (sl + 1) * 128
            for h in range(NH):
                P.dma("sp", lambda e, oTt=oTt, h=h, nt=nt, b0=b0: e.dma_start(
                    out=oT[row0 + h * 128:row0 + (h + 1) * 128, b0:b0 + nt], in_=oTt[0][:, h, 0:nt]), reads=[oTt[1]])
        return oTt

    oTt = None
    for blk in range(int(_os.environ.get("DBG_BLK0", "0")), int(_os.environ.get("DBG_BLK1", str(NB)))):
        oTt = block_body(blk, oTt)


def prog_attn_even(S, NHA, NHB, do_a=True, do_b=True, HG=2):
    nc = bass.Bass("TRN2", target_bir_lowering=False)
    env = Env(nc)
    P = env.P
    common_consts(env)
    oT = env.out("oT", [(NHA + NHB) * 128, S], BF16)
    if do_a:
        qa = env.inp("qa", [S, NHA * 128], BF16)
        ka = env.inp("ka", [S, NHA * 128], BF16)
        va = env.inp("va", [S, NHA * 128], BF16)
        P.push_scope()
        consts = attn_even_consts(env)
        diff_attention(env, S, NHA, qa, ka, va, oT, consts)
        P.pop_scope()
    if do_b:
        W = NHB * 128
        aps = dict(bq=env.inp("bq", [S, W]), bk=env.inp("bk", [S, W]), bv=env.inp("bv", [S, W]), bz=env.inp("bz", [S, W]),
                   bab=env.inp("bab", [S, 2 * NHB]), cw=env.inp("cw", [128, 3, 4, W]), alog=env.inp("alog", [128, NHB]),
                   dtb=env.inp("dtb", [128, NHB]), gout=env.inp("gout", [128, 128]), mU=env.inp("mU", [128, 128]),
                   mC=env.inp("mC", [128, 128]), mB0=env.inp("mB0", [128, 128]), mB1=env.inp("mB1", [128, 128]),
                   MBL=env.inp("MBL", [128, NHB, 128]), MBU=env.inp("MBU", [128, NHB, 128]),
                   NOTI=env.inp("NOTI", [128, NHB, 128]), id32=env.inp("id32", [128, 128]))
        for h0 in range(0, NHB, HG):
            nh = min(HG, NHB - h0)
            P.push_scope()
            gdn(env, S, (NHA + h0) * 128, nh, h0, aps, oT)
            P.pop_scope()
    P.finish()
    return nc


_PROGS = {}


def _prog(name, fn, *args):
    key = (name,) + args
    if key not in _PROGS:
        _PROGS[key] = fn(*args)
    return _PROGS[key]


def _rep(v):
    v = np.asarray(v, np.float32)
    return np.ascontiguousarray(np.broadcast_to(v[None], (128,) + v.shape))


def _rope_consts(pos, hd):
    inv = (1.0 / (10000.0 ** (np.arange(0, hd, 2, dtype=np.float32) / np.float32(hd)))).astype(np.float32)
    ang = pos.astype(np.float32)[:, None] * inv[None, :]
    c = np.cos(ang).astype(np.float32)
    s = np.sin(ang).astype(np.float32)
    cfull = np.concatenate([c, c], 1)
    ssig = np.concatenate([-s, s], 1)
    nb = len(pos) // 128
    return (np.ascontiguousarray(cfull.reshape(nb, 128, hd).transpose(1, 0, 2)),
            np.ascontiguousarray(ssig.reshape(nb, 128, hd).transpose(1, 0, 2)))


def _gdn_masks(NH):
    j = np.arange(128)[:, None]
    i = np.arange(128)[None, :]
    same = (j // 64) == (i // 64)
    t = lambda m: np.ascontiguousarray(np.broadcast_to(m[:, None, :], (128, NH, 128))).astype(np.float32)
    return dict(mU=(same & (j <= i)).astype(np.float32), mC=same.astype(np.float32),
                mB0=np.ascontiguousarray(np.broadcast_to(j < 64, (128, 128))).astype(np.float32),
                mB1=np.ascontiguousarray(np.broadcast_to(j >= 64, (128, 128))).astype(np.float32),
                MBL=t(np.where(same & (j > i), 0.0, -30000.0)), MBU=t(np.where(same & (i >= j), 0.0, -30000.0)),
                NOTI=t((j != i).astype(np.float32)), id32=np.eye(128, dtype=np.float32))


def _run(nc, in_maps):
    res = run_bass_kernel_spmd(nc, in_maps, core_ids=list(range(NB_CORE)))
    return res.results


def kernel(x, ab_norm, ab_w_in, a_q_norm, a_k_norm, a_lambda, a_sub_norm, b_conv, b_a_log,
           b_dt_bias, b_out_norm, ab_w_out, c_norm, c_w_in, c_q_norm, c_k_norm, c_w_out,
           mlp_norm, mlp_w1, mlp_w2):
    A = lambda t: np.ascontiguousarray(np.asarray(t))
    x = A(x).astype(np.float32)
    B, S_, _ = x.shape
    NT = S_ // 2
    ident = np.eye(128, dtype=np.float32)
    cores = [(c // 2, c % 2) for c in range(NB_CORE)]
    jj = np.arange(128)[:, None]
    ii = np.arange(128)[None, :]
    tri = (jj <= ii).astype(np.float32)
    tri2 = np.ascontiguousarray(np.stack([tri, tri], 1))
    prev = (jj >= ii).astype(np.float32)
    mask4 = np.ascontiguousarray(np.stack([prev, tri, prev, tri], 1))
    gm = _gdn_masks(4)
    xs = [A(x[b, hh * NT:(hh + 1) * NT]) for (b, hh) in cores]
    for l in range(4):
        i = l // 2
        if l % 2 == 0:
            nc = _prog("in_even", prog_in_even, NT)
            rc = [_rope_consts(np.arange(hh * NT, (hh + 1) * NT), 64) for hh in range(2)]
            w = A(ab_w_in[i])
            g, gq, gk = _rep(ab_norm[i]), _rep(a_q_norm[i]), _rep(a_k_norm[i])
            ims = [dict(x=xs[c], w=w, ident=ident, g=g, gq=gq, gk=gk, cfull=rc[hh][0], ssig=rc[hh][1])
                   for c, (b, hh) in enumerate(cores)]
            r = _run(nc, ims)
            cat = lambda name, b: np.concatenate([r[2 * b][name], r[2 * b + 1][name]], 0)
            lam_init = 0.8 - 0.6 * math.exp(-0.3 * l)
            lam0 = _rep(np.array([lam_init, 1.0 - lam_init], np.float32))
            lamv = _rep(A(a_lambda[i]).astype(np.float32))
            conv = A(b_conv[i]).astype(np.float32)
            nc2 = _prog("attn_even", prog_attn_even, S_, 4, 4)
            ims = []
            for c, (b, hh) in enumerate(cores):
                qa, ka, va = cat("qa", b), cat("ka", b), cat("va", b)
                bqkv, bz, bab = cat("bqkv", b), cat("bz", b), cat("bab", b)
                cs = slice(hh * 512, (hh + 1) * 512)
                hs = slice(hh * 4, (hh + 1) * 4)
                cw = np.stack([conv[::-1, 0:1024][:, cs], conv[::-1, 1024:2048][:, cs], conv[::-1, 2048:3072][:, cs]], 0)
                ims.append(dict(qa=A(qa[:, cs]), ka=A(ka[:, cs]), va=A(va[:, cs]), ident=ident, tri=tri2, lamv=lamv, lam0=lam0,
                                gsub=A(a_sub_norm[i]).astype(np.float32).reshape(128, 1),
                                bq=A(bqkv[:, 0:1024][:, cs]), bk=A(bqkv[:, 1024:2048][:, cs]), bv=A(bqkv[:, 2048:3072][:, cs]),
                                bz=A(bz[:, cs]), bab=A(np.concatenate([bab[:, 0:8][:, hs], bab[:, 8:16][:, hs]], 1)),
                                cw=_rep(cw), alog=_rep(A(b_a_log[i])[hs]), dtb=_rep(A(b_dt_bias[i])[hs]),
                                gout=_rep(b_out_norm[i]), **gm))
            r2 = _run(nc2, ims)
            oTs = []
            for b in range(B):
                oTs.append(np.concatenate([r2[2 * b]["oT"][0:512], r2[2 * b + 1]["oT"][0:512],
                                           r2[2 * b]["oT"][512:1024], r2[2 * b + 1]["oT"][512:1024]], 0))
            wo = A(ab_w_out[i])
        else:
            nc = _prog("in_odd", prog_in_odd, NT)
            rc = [_rope_consts(np.arange(hh * NT, (hh + 1) * NT), 128) for hh in range(2)]
            w = A(c_w_in[i])
            g, gq, gk = _rep(c_norm[i]), _rep(c_q_norm[i]), _rep(c_k_norm[i])
            ims = [dict(x=xs[c], w=w, ident=ident, g=g, gq=gq, gk=gk, cfull=rc[hh][0], ssig=rc[hh][1])
                   for c, (b, hh) in enumerate(cores)]
            r = _run(nc, ims)
            cat = lambda name, b: np.concatenate([r[2 * b][name], r[2 * b + 1][name]], 0)
            nc2 = _prog("attn_odd", prog_attn_odd, S_, 8)
            ims = []
            for c, (b, hh) in enumerate(cores):
                cs = slice(hh * 1024, (hh + 1) * 1024)
                ims.append(dict(q=A(cat("qc", b)[:, cs]), k=A(cat("kc", b)[:, cs]), v=A(cat("vc", b)[:, cs]),
                                ident=ident, mask=mask4))
            r2 = _run(nc2, ims)
            oTs = [np.concatenate([r2[2 * b]["oT"], r2[2 * b + 1]["oT"]], 0) for b in range(B)]
            wo = A(c_w_out[i])
        nc3 = _prog("post", prog_post, NT, 512)
        gm_ = _rep(mlp_norm[l])
        w1, w2 = A(mlp_w1[l]), A(mlp_w2[l])
        ims = [dict(oT=A(oTs[b][:, hh * NT:(hh + 1) * NT]), x=xs[c], wo=wo, w1=w1, w2=w2, ident=ident, g=gm_)
               for c, (b, hh) in enumerate(cores)]
        r3 = _run(nc3, ims)
        xs = [r3[c]["xout"] for c in range(NB_CORE)]
    out = np.empty((B, S_, D), np.float32)
    for c, (b, hh) in enumerate(cores):
        out[b, hh * NT:(hh + 1) * NT] = xs[c]
    return out
```
